# Optimizing a Trainium2 kernel written in Bass

```python
import math
import jax, jax.numpy as jnp
from jax import lax
import numpy as np

D_MODEL = 1024
BATCH = 16
SEQ = 2048
DEPTH = 4

N_MIXERS = 2
ATTN_HEADS = 8
ATTN_HEAD_DIM = 64
ATTN_WIDTH = ATTN_HEADS * 2 * ATTN_HEAD_DIM
Q_BLOCK = 128
LRU_WIDTH = D_MODEL
LRU_BLOCKS = 4
LRU_BLOCK_DIM = LRU_WIDTH // LRU_BLOCKS
CONV_WIDTH = 4
LRU_C = 8.0
NORM_EPS = 1e-6
N_ATTN_LAYERS = (DEPTH + N_MIXERS - 1) // N_MIXERS
N_LRU_LAYERS = DEPTH // N_MIXERS

kernel_name = "hybrid_diffattn_rglru_sandwich"


def rms_norm(x, w):
    xf = x.astype(jnp.float32)
    y = xf * lax.rsqrt(jnp.mean(xf * xf, axis=-1, keepdims=True) + NORM_EPS)
    return (y * w.astype(jnp.float32)).astype(x.dtype)


def alibi_slopes(n_heads):
    h = jnp.arange(1, n_heads + 1, dtype=jnp.float32)
    return jnp.exp2(-(8.0 / n_heads) * h)


def diff_attention_core(q, k, v, lam):
    B, H, _, S, d = q.shape
    n_blk = S // Q_BLOCK
    scale = d ** -0.5
    slopes = alibi_slopes(H)
    q_blocks = q.reshape(B, H, 2, n_blk, Q_BLOCK, d).transpose(3, 0, 1, 2, 4, 5)
    key_pos = jnp.arange(S)

    def one_block(args):
        q_blk, blk_idx = args
        scores = jnp.einsum('bhmqd,bhmkd->bhmqk', q_blk, k).astype(jnp.float32) * scale
        query_pos = blk_idx * Q_BLOCK + jnp.arange(Q_BLOCK)
        dist = (query_pos[:, None] - key_pos[None, :]).astype(jnp.float32)
        bias = -slopes[:, None, None, None] * dist[None, None]
        scores = jnp.where(dist >= 0, scores + bias, -jnp.inf)
        probs = jax.nn.softmax(scores, axis=-1)
        weights = probs[:, :, 0] - lam * probs[:, :, 1]
        return jnp.einsum('bhqk,bhke->bhqe', weights.astype(v.dtype), v)

    out = lax.map(one_block, (q_blocks, jnp.arange(n_blk)))
    return out.transpose(1, 0, 3, 2, 4).reshape(B, S, H, 2 * d)


def diff_attention_mixer(h, w_in, w_out, lq1, lk1, lq2, lk2, subln_w, lam_init):
    B, S, _ = h.shape
    H, d = ATTN_HEADS, ATTN_HEAD_DIM
    proj = h @ w_in
    q, k, v, gate = jnp.split(proj, 4, axis=-1)
    q = q.reshape(B, S, H, 2, d).transpose(0, 2, 3, 1, 4)
    k = k.reshape(B, S, H, 2, d).transpose(0, 2, 3, 1, 4)
    v = v.reshape(B, S, H, 2 * d).transpose(0, 2, 1, 3)
    lam = (jnp.exp(jnp.sum(lq1.astype(jnp.float32) * lk1.astype(jnp.float32)))
           - jnp.exp(jnp.sum(lq2.astype(jnp.float32) * lk2.astype(jnp.float32)))
           + lam_init)
    o = diff_attention_core(q, k, v, lam)
    o = rms_norm(o, subln_w) * (1.0 - lam_init)
    o = o.reshape(B, S, ATTN_WIDTH)
    return (o * jax.nn.silu(gate)) @ w_out


def causal_depthwise_conv(x, w, b):
    C = x.shape[-1]
    y = lax.conv_general_dilated(
        x, w[:, None, :].astype(x.dtype), window_strides=(1,),
        padding=[(CONV_WIDTH - 1, 0)], dimension_numbers=('NWC', 'WIO', 'NWC'),
        feature_group_count=C)
    return y + b


def block_diag_linear(x, w, b):
    B, S, _ = x.shape
    xb = x.reshape(B, S, LRU_BLOCKS, LRU_BLOCK_DIM)
    y = jnp.einsum('bsgi,gij->bsgj', xb, w).reshape(B, S, LRU_WIDTH)
    return y + b


def rg_lru(x, gate_a_w, gate_a_b, gate_x_w, gate_x_b, log_a_param):
    r = jax.nn.sigmoid(block_diag_linear(x, gate_a_w, gate_a_b).astype(jnp.float32))
    i = jax.nn.sigmoid(block_diag_linear(x, gate_x_w, gate_x_b).astype(jnp.float32))
    log_a = -LRU_C * r * jax.nn.softplus(-log_a_param.astype(jnp.float32))
    a = jnp.exp(log_a)
    mult = jnp.sqrt(-jnp.expm1(2.0 * log_a))
    b = mult * (i * x.astype(jnp.float32))

    def combine(left, right):
        a_l, b_l = left
        a_r, b_r = right
        return a_l * a_r, a_r * b_l + b_r

    _, h = lax.associative_scan(combine, (a, b), axis=1)
    return h.astype(x.dtype)


def rglru_mixer(h, w_in, conv_w, conv_b, gate_a_w, gate_a_b, gate_x_w, gate_x_b,
                log_a_param, w_out):
    proj = h @ w_in
    xr, gate = jnp.split(proj, 2, axis=-1)
    xr = causal_depthwise_conv(xr, conv_w, conv_b)
    y = rg_lru(xr, gate_a_w, gate_a_b, gate_x_w, gate_x_b, log_a_param)
    return (y * jax.nn.silu(gate)) @ w_out


def setup_inputs(seed: int = 0) -> dict:
    key = jax.random.key(seed)
    ks = jax.random.split(key, 20)
    f32 = jnp.float32
    D, NA, NL = D_MODEL, N_ATTN_LAYERS, N_LRU_LAYERS
    x = jax.random.normal(ks[0], (BATCH, SEQ, D), f32)
    pre_norm_w = 1.0 + 0.05 * jax.random.normal(ks[1], (DEPTH, D), f32)
    post_norm_w = 1.0 + 0.05 * jax.random.normal(ks[2], (DEPTH, D), f32)
    attn_w_in = jax.random.normal(ks[3], (NA, D, 4 * ATTN_WIDTH), f32) * D ** -0.5
    attn_w_out = jax.random.normal(ks[4], (NA, ATTN_WIDTH, D), f32) * ATTN_WIDTH ** -0.5
    attn_lambda_q1 = 0.1 * jax.random.normal(ks[5], (NA, ATTN_HEAD_DIM), f32)
    attn_lambda_k1 = 0.1 * jax.random.normal(ks[6], (NA, ATTN_HEAD_DIM), f32)
    attn_lambda_q2 = 0.1 * jax.random.normal(ks[7], (NA, ATTN_HEAD_DIM), f32)
    attn_lambda_k2 = 0.1 * jax.random.normal(ks[8], (NA, ATTN_HEAD_DIM), f32)
    attn_subln_w = 1.0 + 0.05 * jax.random.normal(ks[9], (NA, 2 * ATTN_HEAD_DIM), f32)
    lru_w_in = jax.random.normal(ks[10], (NL, D, 2 * LRU_WIDTH), f32) * D ** -0.5
    lru_conv_w = jax.random.normal(ks[11], (NL, CONV_WIDTH, LRU_WIDTH), f32) * CONV_WIDTH ** -0.5
    lru_conv_b = 0.01 * jax.random.normal(ks[12], (NL, LRU_WIDTH), f32)
    lru_gate_a_w = jax.random.normal(ks[13], (NL, LRU_BLOCKS, LRU_BLOCK_DIM, LRU_BLOCK_DIM), f32) * LRU_BLOCK_DIM ** -0.5
    lru_gate_a_b = 0.01 * jax.random.normal(ks[14], (NL, LRU_WIDTH), f32)
    lru_gate_x_w = jax.random.normal(ks[15], (NL, LRU_BLOCKS, LRU_BLOCK_DIM, LRU_BLOCK_DIM), f32) * LRU_BLOCK_DIM ** -0.5
    lru_gate_x_b = 0.01 * jax.random.normal(ks[16], (NL, LRU_WIDTH), f32)
    a_c = jax.random.uniform(ks[17], (NL, LRU_WIDTH), f32, 0.9, 0.999)
    s = a_c ** (1.0 / LRU_C)
    lru_log_a_param = jnp.log(s) - jnp.log1p(-s)
    lru_w_out = jax.random.normal(ks[18], (NL, LRU_WIDTH, D), f32) * LRU_WIDTH ** -0.5
    return {
        "x": x, "pre_norm_w": pre_norm_w, "post_norm_w": post_norm_w,
        "attn_w_in": attn_w_in, "attn_w_out": attn_w_out,
        "attn_lambda_q1": attn_lambda_q1, "attn_lambda_k1": attn_lambda_k1,
        "attn_lambda_q2": attn_lambda_q2, "attn_lambda_k2": attn_lambda_k2,
        "attn_subln_w": attn_subln_w,
        "lru_w_in": lru_w_in, "lru_conv_w": lru_conv_w, "lru_conv_b": lru_conv_b,
        "lru_gate_a_w": lru_gate_a_w, "lru_gate_a_b": lru_gate_a_b,
        "lru_gate_x_w": lru_gate_x_w, "lru_gate_x_b": lru_gate_x_b,
        "lru_log_a_param": lru_log_a_param, "lru_w_out": lru_w_out,
    }


def reference(x, pre_norm_w, post_norm_w, attn_w_in, attn_w_out, attn_lambda_q1,
              attn_lambda_k1, attn_lambda_q2, attn_lambda_k2, attn_subln_w,
              lru_w_in, lru_conv_w, lru_conv_b, lru_gate_a_w, lru_gate_a_b,
              lru_gate_x_w, lru_gate_x_b, lru_log_a_param, lru_w_out):
    for layer in range(DEPTH):
        h = rms_norm(x, pre_norm_w[layer])
        j = layer // N_MIXERS
        if layer % N_MIXERS == 0:
            lam_init = 0.8 - 0.6 * math.exp(-0.3 * layer)
            out = diff_attention_mixer(
                h, attn_w_in[j], attn_w_out[j], attn_lambda_q1[j], attn_lambda_k1[j],
                attn_lambda_q2[j], attn_lambda_k2[j], attn_subln_w[j], lam_init)
        else:
            out = rglru_mixer(
                h, lru_w_in[j], lru_conv_w[j], lru_conv_b[j], lru_gate_a_w[j],
                lru_gate_a_b[j], lru_gate_x_w[j], lru_gate_x_b[j],
                lru_log_a_param[j], lru_w_out[j])
        x = x + rms_norm(out, post_norm_w[layer])
    return x
```

```python
import math
from contextlib import ExitStack

import numpy as np
import ml_dtypes
import concourse.bass as bass
import concourse.mybir as mybir
from concourse.alu_op_type import AluOpType as ALU
from concourse.bass_utils import run_bass_kernel_spmd

F32 = mybir.dt.float32
BF16 = mybir.dt.bfloat16
AF = mybir.ActivationFunctionType

S = 2048
D = 1024
NT = 16
KC = 8
H = 8
EPS = 1e-6
N_CORES = 8
SEQ_PER_CORE = 2
DMA_SCRATCH = 512
INTERLEAVE = True
SPLIT_K = False

C_PREW = 0
C_SUBLN = 32
C_CONVW = 34
C_CONVB = 98
C_GAB = 114
C_GXB = 130
C_LOGA = 146
C_TOTAL = 162


class Ev:
    __slots__ = ("sem", "key", "val", "eng")

    def __init__(self, sem, key, val, eng):
        self.sem, self.key, self.val, self.eng = sem, key, val, eng


class Buf:
    __slots__ = ("name", "w", "r")

    def __init__(self, name):
        self.name = name
        self.w = None
        self.r = {}


class Eng:
    def __init__(self, name, h, sem):
        self.name, self.h, self.sem = name, h, sem
        self.cnt = 0
        self.waited = {}


class DSem:
    def __init__(self, name, sem):
        self.name, self.sem, self.val = name, sem, 0


class Ctx:
    def __init__(self, nc, es):
        self.nc = nc
        self.es = es
        self.pe = Eng("pe", nc.tensor, es.enter_context(nc.semaphore("s_pe")))
        self.act = Eng("act", nc.scalar, es.enter_context(nc.semaphore("s_act")))
        self.dve = Eng("dve", nc.vector, es.enter_context(nc.semaphore("s_dve")))
        self.pool = Eng("pool", nc.gpsimd, es.enter_context(nc.semaphore("s_pool")))
        self.sp = Eng("sp", nc.sync, es.enter_context(nc.semaphore("s_sp")))
        self.engs = [self.pe, self.act, self.dve, self.pool, self.sp]
        self.dsems = {}
        self.bufs = {}

    def buf(self, name):
        b = self.bufs.get(name)
        if b is None:
            b = self.bufs[name] = Buf(name)
        return b

    def dsem(self, name):
        d = self.dsems.get(name)
        if d is None:
            d = self.dsems[name] = DSem(name, self.es.enter_context(self.nc.semaphore("d_" + name)))
        return d

    def _wait(self, eng, ev):
        if ev is None:
            return
        if ev.eng == "pe" and eng.name == "pe":
            return
        if eng.waited.get(ev.key, 0) >= ev.val:
            return
        eng.h.wait_ge(ev.sem, ev.val)
        eng.waited[ev.key] = ev.val

    def _deps(self, eng, reads, writes):
        for b in reads:
            self._wait(eng, b.w)
        for b in writes:
            self._wait(eng, b.w)
            for ev in b.r.values():
                self._wait(eng, ev)

    def _mark(self, ev, rkey, reads, writes):
        for b in reads:
            b.r[rkey] = ev
        for b in writes:
            b.w = ev
            b.r = {}

    def op(self, eng, fn, reads=(), writes=()):
        self._deps(eng, reads, writes)
        inst = fn()
        eng.cnt += 1
        inst.then_inc(eng.sem, 1)
        ev = Ev(eng.sem, eng.name, eng.cnt, eng.name)
        eng.waited[eng.name] = max(eng.waited.get(eng.name, 0), 0)
        self._mark(ev, eng.name, reads, writes)
        return ev

    def group(self, eng, fns, reads=(), writes=()):
        self._deps(eng, reads, writes)
        inst = None
        for fn in fns:
            inst = fn()
        eng.cnt += 1
        inst.then_inc(eng.sem, 1)
        ev = Ev(eng.sem, eng.name, eng.cnt, eng.name)
        self._mark(ev, eng.name, reads, writes)
        return ev

    def dma(self, dsem, pairs, reads=(), writes=()):
        q = self.sp
        self._deps(q, reads, writes)
        for o, i in pairs:
            q.h.dma_start(out=o, in_=i).then_inc(dsem.sem, 16)
            dsem.val += 16
        ev = Ev(dsem.sem, "d_" + dsem.name, dsem.val, "dma")
        self._mark(ev, "d_" + dsem.name, reads, writes)
        return ev

    def barrier(self):
        evs = []
        for e in self.engs:
            if e.cnt > 0:
                evs.append(Ev(e.sem, e.name, e.cnt, "x"))
        for d in self.dsems.values():
            if d.val > 0:
                evs.append(Ev(d.sem, "d_" + d.name, d.val, "dma"))
        for e in self.engs:
            for ev in evs:
                if e.waited.get(ev.key, 0) >= ev.val:
                    continue
                if ev.key == e.name and e.name == "sp":
                    continue
                e.h.wait_ge(ev.sem, ev.val)
                e.waited[ev.key] = ev.val
        for b in self.bufs.values():
            b.w = None
            b.r = {}


def build_nc(layers=(0, 1, 2, 3), nseq=SEQ_PER_CORE):
    nc = bass.Bass("TRN2", target_bir_lowering=False, dynamic_dma_scratch_size=DMA_SCRATCH)
    dt = nc.dram_tensor
    x_d = dt("x", [SEQ_PER_CORE, S, D], F32, kind="ExternalInput").ap()
    y_d = dt("y", [SEQ_PER_CORE, S, D], F32, kind="ExternalOutput").ap()
    awin_d = dt("attn_w_in", [2, D, 4 * D], F32, kind="ExternalInput").ap()
    awout_d = dt("attn_w_out", [2, D, D], F32, kind="ExternalInput").ap()
    lwin_d = dt("lru_w_in", [2, D, 2 * D], F32, kind="ExternalInput").ap()
    lwout_d = dt("lru_w_out", [2, D, D], F32, kind="ExternalInput").ap()
    gaw_d = dt("lru_gate_a_w", [2, 4, 256, 256], F32, kind="ExternalInput").ap()
    gxw_d = dt("lru_gate_x_w", [2, 4, 256, 256], F32, kind="ExternalInput").ap()
    cols_d = dt("cols", [128, C_TOTAL], F32, kind="ExternalInput").ap()
    postw_d = dt("postw", [4, 128, D], F32, kind="ExternalInput").ap()
    lamrows_d = dt("lamrows", [128, 2, 4, 64], F32, kind="ExternalInput").ap()
    prewr_d = dt("prewr", [4, 128, D], F32, kind="ExternalInput").ap()
    tri_d = dt("tri", [128, 2, 128], BF16, kind="ExternalInput").ap()
    maskneg_d = dt("maskneg", [128, 128], BF16, kind="ExternalInput").ap()
    ident_d = dt("ident", [128, 128], BF16, kind="ExternalInput").ap()
    augq_d = dt("augq", [H, 4, S], BF16, kind="ExternalInput").ap()
    augk_d = dt("augk", [H, 4, S], BF16, kind="ExternalInput").ap()

    with ExitStack() as es:
        cx = Ctx(nc, es)
        pe, act, dve, pool = cx.pe, cx.act, cx.dve, cx.pool
        B = cx.buf

        uniq = [0]

        def sbt(name, shape, dtype):
            uniq[0] += 1
            return nc.sbuf_tensor(f"{name}_{uniq[0]}", shape, dtype)

        def sb(name, shape, dtype):
            return es.enter_context(sbt(name, shape, dtype))

        X = sb("X", [128, NT, D], F32)
        HT = sb("HT", [128, KC, S], BF16)
        SG = sb("SG", [128, NT, D], BF16)
        YGT = SG[:].rearrange("p t d -> p (t d)").rearrange("p (c s) -> p c s", c=KC)
        STG = [sb(f"stg{i}", [128, 1024], F32) for i in range(2)]
        COLS = sb("COLS", [128, C_TOTAL], F32)
        IDENT = sb("IDENT", [128, 128], BF16)
        TRI = sb("TRI", [128, 2, 128], BF16)
        MASKNEG = sb("MASKNEG", [128, 128], BF16)
        NEGLAM = sb("NEGLAM", [128, 2], F32)
        SUBC = sb("SUBC", [128, 4], F32)
        SS = sb("SS", [128, NT], F32)
        RSTD = sb("RSTD", [128, NT], F32)
        LNV = sb("LNV", [128, NT], F32)
        SS2 = sb("SS2", [128, 4], F32)
        CCS = sb("CCS", [128, 2, KC, 2], F32)
        HBS = sb("HBS", [128, 2, KC, 2], F32)
        ONEC = sb("ONEC", [128, 1], F32)
        PS = es.enter_context(nc.psum_tensor("PS", [128, 8, 512], F32))

        def psb(b):
            return PS[:, b, :]

        def psb16(b):
            return PS[:, b, :].bitcast(BF16)

        stg_i = [0]

        class BgLoader:
            def __init__(self):
                self.jobs = []
                self.inflight = []
                self.cur = 0

            def add(self, dst, src, n, W, scale=None, name="w", tag=None):
                self.jobs.append(dict(dst=dst, src=src, n=n, W=W, scale=scale, name=name, tag=tag))

            def _cast(self, jb):
                i, st, dst, scale = jb["i"], jb["st"], jb["dst"], jb["scale"]
                if scale is None:
                    fn = lambda: nc.vector.tensor_copy(out=dst, in_=st)
                else:
                    fn = lambda: nc.vector.tensor_scalar(out=dst, in0=st, scalar1=scale, scalar2=None, op0=ALU.mult)
                cx.op(dve, fn, reads=[B(f"stg{i}")] + ([B("scw")] if scale is not None else []),
                      writes=[B(jb["name"])])

            def step(self):
                ready = [k for k, jb in enumerate(self.jobs)
                         if not isinstance(jb["tag"], int) or jb["tag"] - 2 <= self.cur]
                if len(self.inflight) == 2 or (self.inflight and not ready):
                    self._cast(self.inflight.pop(0))
                if ready and len(self.inflight) < 2:
                    jb = self.jobs.pop(ready[0])
                    i = stg_i[0] % 2
                    stg_i[0] += 1
                    st = STG[i][:, 0:jb["n"] * jb["W"]].rearrange("p (n w) -> p n w", n=jb["n"])
                    cx.dma(cx.dsem(f"stg{i}"), [(st, jb["src"])], writes=[B(f"stg{i}")])
                    jb["i"], jb["st"] = i, st
                    self.inflight.append(jb)

            def flush(self, tag=None):
                def pending():
                    if tag is None:
                        return bool(self.jobs or self.inflight)
                    return any(j["tag"] == tag for j in self.jobs + self.inflight)
                if tag is None:
                    self.cur = 10 ** 6
                guard = 0
                while pending():
                    self.step()
                    guard += 1
                    assert guard < 1000, "BgLoader.flush stuck"

        def load_w(dst, src, n, W, eng=None, scale=None, name="w"):
            eng = eng or dve
            i = stg_i[0] % 2
            stg_i[0] += 1
            st = STG[i][:, 0:n * W].rearrange("p (n w) -> p n w", n=n)
            cx.dma(cx.dsem(f"stg{i}"), [(st, src)], writes=[B(f"stg{i}")])
            if eng is dve:
                if scale is None:
                    fn = lambda: nc.vector.tensor_copy(out=dst, in_=st)
                else:
                    fn = lambda: nc.vector.tensor_scalar(out=dst, in0=st, scalar1=scale, scalar2=None, op0=ALU.mult)
            else:
                if scale is None:
                    fn = lambda: nc.scalar.copy(out=dst, in_=st)
                else:
                    fn = lambda: nc.scalar.activation(out=dst, in_=st, func=AF.Copy, scale=scale)
            cx.op(eng, fn, reads=[B(f"stg{i}")] + ([B("scw")] if scale is not None else []), writes=[B(name)])

        cx.dma(cx.dsem("c0"), [(COLS[:], cols_d), (IDENT[:], ident_d), (TRI[:], tri_d), (MASKNEG[:], maskneg_d)],
               writes=[B("consts")])
        with sbt("LAMR", [128, 2, 4, 64], F32) as LAMR, \
                sbt("LAMS", [128, 4], F32) as LAMS, \
                sbt("LAMJ", [128, 64], F32) as LAMJ:
            cx.dma(cx.dsem("c1"), [(LAMR[:], lamrows_d)], writes=[B("lamr")])
            for j in range(2):
                layer = 2 * j
                lam_init = 0.8 - 0.6 * math.exp(-0.3 * layer)
                for q in range(2):
                    cx.op(dve, lambda j=j, q=q: nc.vector.tensor_tensor(
                        out=LAMJ[:], in0=LAMR[:, j, 2 * q, :], in1=LAMR[:, j, 2 * q + 1, :], op=ALU.mult),
                        reads=[B("lamr")], writes=[B("lamj")])
                    cx.op(dve, lambda j=j, q=q: nc.vector.tensor_reduce(
                        out=LAMS[:, 2 * j + q:2 * j + q + 1], in_=LAMJ[:], axis=mybir.AxisListType.X, op=ALU.add),
                        reads=[B("lamj")], writes=[B("lams")])
                cx.op(act, lambda j=j: nc.scalar.activation(
                    out=LAMS[:, 2 * j:2 * j + 2], in_=LAMS[:, 2 * j:2 * j + 2], func=AF.Exp),
                    reads=[B("lams")], writes=[B("lams")])
                cx.op(dve, lambda j=j: nc.vector.tensor_tensor(
                    out=NEGLAM[:, j:j + 1], in0=LAMS[:, 2 * j + 1:2 * j + 2], in1=LAMS[:, 2 * j:2 * j + 1],
                    op=ALU.subtract), reads=[B("lams")], writes=[B("neglam")])
                cx.op(dve, lambda j=j, li=lam_init: nc.vector.tensor_scalar(
                    out=NEGLAM[:, j:j + 1], in0=NEGLAM[:, j:j + 1], scalar1=-li, scalar2=None, op0=ALU.add),
                    reads=[B("neglam")], writes=[B("neglam")])
                cx.op(dve, lambda j=j, li=lam_init: nc.vector.tensor_scalar(
                    out=SUBC[:, j:j + 1], in0=COLS[:, C_SUBLN + j:C_SUBLN + j + 1], scalar1=1.0 - li,
                    scalar2=None, op0=ALU.mult), reads=[B("consts")], writes=[B("subc")])
            cx.barrier()

        def prenorm_ops(HN, JUNK, PREWR, copy_eng=None):
            def a(t):
                cx.op(act, lambda: nc.scalar.activation(
                    out=JUNK[:], in_=X[:, t, :], func=AF.Square, accum_out=SS[:, t:t + 1]),
                    reads=[B(f"X{t}")], writes=[B("junk"), B(f"ss{t}")])
                cx.op(act, lambda: nc.scalar.activation(
                    out=LNV[:, t:t + 1], in_=SS[:, t:t + 1], func=AF.Ln, bias=EPSC[:, 0:1], scale=1.0 / D),
                    reads=[B(f"ss{t}")], writes=[B(f"lnv{t}")])
                cx.op(act, lambda: nc.scalar.activation(
                    out=RSTD[:, t:t + 1], in_=LNV[:, t:t + 1], func=AF.Exp, scale=-0.5),
                    reads=[B(f"lnv{t}")], writes=[B(f"rstd{t}")])

            def b(t):
                hn = HN[t % 2]
                cx.op(dve, lambda: nc.vector.scalar_tensor_tensor(
                    out=hn[:], in0=X[:, t, :], scalar=RSTD[:, t:t + 1], in1=PREWR[:], op0=ALU.mult, op1=ALU.mult),
                    reads=[B(f"X{t}"), B(f"rstd{t}"), B("prewr")], writes=[B(f"hn{t % 2}")])
                bank = 4 + (t % 2)
                pt = psb16(bank).rearrange("p (k c) -> p k c", k=KC)
                cx.group(pe, [lambda k=k: nc.tensor.transpose(
                    pt[:, k, :], hn[:, k * 128:(k + 1) * 128], IDENT[:]) for k in range(KC)],
                    reads=[B(f"hn{t % 2}"), B("consts")], writes=[B(f"ps{bank}")])

            def c(t):
                bank = 4 + (t % 2)
                pt = psb16(bank).rearrange("p (k c) -> p k c", k=KC)
                if copy_eng is act:
                    cx.op(act, lambda: nc.scalar.copy(out=HT[:, :, t * 128:(t + 1) * 128], in_=pt),
                          reads=[B(f"ps{bank}")], writes=[B(f"HT{t}")])
                else:
                    cx.op(dve, lambda: nc.vector.tensor_copy(out=HT[:, :, t * 128:(t + 1) * 128], in_=pt),
                          reads=[B(f"ps{bank}")], writes=[B(f"HT{t}")])
            return a, b, c

        def prenorm(layer):
            with sbt("hn0", [128, D], BF16) as HN0, sbt("hn1", [128, D], BF16) as HN1, \
                    sbt("JUNK", [128, D], BF16) as JUNK, sbt("PREWR", [128, D], F32) as PREWR:
                cx.dma(cx.dsem("prewr"), [(PREWR[:], prewr_d[layer])], writes=[B("prewr")])
                a, b, c = prenorm_ops([HN0, HN1], JUNK, PREWR)
                for t in range(NT):
                    a(t)
                b(0)
                for t in range(NT):
                    if t + 1 < NT:
                        b(t + 1)
                    c(t)
                cx.barrier()

        def load_wout(WOUT, wout_d, scale_cols, weng):
            wv = wout_d.rearrange("(kc p) c -> p kc c", p=128)
            for kc in range(KC):
                load_w(WOUT[:, kc:kc + 1, :], wv[:, kc:kc + 1, :], 1, D, eng=weng,
                       scale=None if scale_cols is None else scale_cols[:, 0:1], name=f"wout{kc}")

        def out_proj_post(layer, wout_d, lhs_fn, lhs_reads_fn, scale_cols, weng=None, WOUT=None,
                          next_layer=None, store=None, pre_hook=None):
            with ExitStack() as es3:
                POSTW = es3.enter_context(sbt("POSTW", [128, D], F32))
                JUNK = es3.enter_context(sbt("JUNK2", [128, D], BF16))
                TMPY = [es3.enter_context(sbt(f"tmpy{i}", [128, D], F32)) for i in range(2)]
                cx.dma(cx.dsem("postw"), [(POSTW[:], postw_d[layer])], writes=[B("postw")])
                pa = pb = pc = None
                if next_layer is not None:
                    HN = [es3.enter_context(sbt(f"hn{i}", [128, D], BF16)) for i in range(2)]
                    PREWR = es3.enter_context(sbt("PREWR", [128, D], F32))
                    cx.dma(cx.dsem("prewr"), [(PREWR[:], prewr_d[next_layer])], writes=[B("prewr")])
                    pa, pb, pc = prenorm_ops(HN, JUNK, PREWR)
                if WOUT is None:
                    WOUT = es3.enter_context(sbt("WOUT", [128, KC, D], BF16))
                    XSTG = [es3.enter_context(sbt(f"xstg{i}", [128, D], F32)) for i in range(4)]
                    wv = wout_d.rearrange("(kc p) c -> p kc c", p=128)
                    for kc in range(KC):
                        st = XSTG[kc % 4]
                        cx.dma(cx.dsem(f"xstg{kc % 4}"), [(st[:], wv[:, kc, :])], writes=[B(f"xstg{kc % 4}")])
                        if kc % 2 == 0:
                            cx.op(act, lambda kc=kc, st=st: nc.scalar.copy(out=WOUT[:, kc, :], in_=st[:]),
                                  reads=[B(f"xstg{kc % 4}")], writes=[B(f"wout{kc}")])
                        else:
                            cx.op(dve, lambda kc=kc, st=st: nc.vector.tensor_copy(out=WOUT[:, kc, :], in_=st[:]),
                                  reads=[B(f"xstg{kc % 4}")], writes=[B(f"wout{kc}")])
                if pre_hook is not None:
                    pre_hook()
                nxt_reads = lhs_reads_fn(0)
                for t in range(NT):
                    reads = nxt_reads
                    b0 = 2 * (t % 2)
                    for half in range(2):
                        bank = b0 + half
                        cx.group(pe, [lambda kc=kc, t=t, half=half, bank=bank: nc.tensor.matmul(
                            psb(bank), lhs_fn(t, kc), WOUT[:, kc, half * 512:(half + 1) * 512],
                            start=(kc == 0), stop=(kc == KC - 1)) for kc in range(KC)],
                            reads=reads + [B(f"wout{kc}") for kc in range(KC)], writes=[B(f"ps{bank}")])
                    if pb is not None:
                        if t >= 2:
                            pb(t - 2)
                        if t >= 3:
                            pc(t - 3)
                    if t + 1 < NT:
                        nxt_reads = lhs_reads_fn(t + 1)
                    py = PS[:, b0:b0 + 2, :].rearrange("p b c -> p (b c)")
                    pyb = [B(f"ps{b0}"), B(f"ps{b0 + 1}")]
                    c2 = t % 4
                    cx.op(act, lambda py=py, c2=c2: nc.scalar.activation(
                        out=JUNK[:], in_=py, func=AF.Square, accum_out=SS2[:, c2:c2 + 1]),
                        reads=pyb, writes=[B("junk"), B(f"ss2_{c2}")])
                    cx.op(act, lambda c2=c2: nc.scalar.activation(
                        out=SS2[:, c2:c2 + 1], in_=SS2[:, c2:c2 + 1], func=AF.Ln, bias=EPSC[:, 0:1], scale=1.0 / D),
                        reads=[B(f"ss2_{c2}")], writes=[B(f"ss2_{c2}")])
                    cx.op(act, lambda c2=c2: nc.scalar.activation(
                        out=SS2[:, c2:c2 + 1], in_=SS2[:, c2:c2 + 1], func=AF.Exp, scale=-0.5),
                        reads=[B(f"ss2_{c2}")], writes=[B(f"ss2_{c2}")])
                    if pa is not None and t >= 1:
                        pa(t - 1)
                    ty = TMPY[t % 2]
                    cx.op(dve, lambda py=py, ty=ty, c2=c2: nc.vector.scalar_tensor_tensor(
                        out=ty[:], in0=py, scalar=SS2[:, c2:c2 + 1], in1=POSTW[:], op0=ALU.mult, op1=ALU.mult),
                        reads=pyb + [B(f"ss2_{c2}"), B("postw")], writes=[B(f"tmpy{t % 2}")])
                    cx.op(dve, lambda t=t, ty=ty: nc.vector.tensor_tensor(
                        out=X[:, t, :], in0=X[:, t, :], in1=ty[:], op=ALU.add),
                        reads=[B(f"tmpy{t % 2}")], writes=[B(f"X{t}")])
                    if store is not None:
                        store(t)
                if pa is not None:
                    pa(NT - 1)
                    pb(NT - 2)
                    pc(NT - 3)
                    pb(NT - 1)
                    pc(NT - 2)
                    pc(NT - 1)
                cx.barrier()

        def alloc_wbig():
            st = ExitStack()
            WBIG = st.enter_context(sbt("WBIG", [128, KC, D], BF16))
            return st, WBIG

        def load_gate_w(layer, WBIG):
            win = awin_d[layer // 2].rearrange("(kc p) c -> p kc c", p=128)
            for half in range(2):
                c0 = 3 * D + half * 512
                for kc in range(0, KC, 2):
                    load_w(WBIG[:, kc:kc + 2, half * 512:(half + 1) * 512], win[:, kc:kc + 2, c0:c0 + 512],
                           2, 512, name=f"wg{half}_{kc}")

        def attn_layer(layer, prenormed=False, wb=None, next_layer=None, store=None):
            j = layer // 2
            win = awin_d[j].rearrange("(kc p) c -> p kc c", p=128)
            if wb is None:
                wb = alloc_wbig()
                load_gate_w(layer, wb[0 + 1])
            wb_stack, WBIG = wb
            with ExitStack() as esl:
                SCW = esl.enter_context(sbt("SCW", [128, 1], F32))
                cx.op(dve, lambda: nc.vector.tensor_copy(out=SCW[:], in_=SUBC[:, j:j + 1]), writes=[B("scw")])
                if not prenormed:
                    prenorm(layer)
                else:
                    cx.barrier()
                for half in range(2):
                    for t in range(NT):
                        bank = 6 + (t % 2)
                        cx.group(pe, [lambda kc=kc, t=t, bank=bank, half=half: nc.tensor.matmul(
                            psb(bank), HT[:, kc, t * 128:(t + 1) * 128], WBIG[:, kc, half * 512:(half + 1) * 512],
                            start=(kc == 0), stop=(kc == KC - 1)) for kc in range(KC)],
                            reads=[B(f"HT{t}")], writes=[B(f"ps{bank}")])
                        cx.op(act, lambda t=t, half=half, bank=bank: nc.scalar.activation(
                            out=SG[:, t, half * 512:(half + 1) * 512], in_=psb(bank), func=AF.Silu),
                            reads=[B(f"ps{bank}")], writes=[B(f"SG{t}")])
                cx.barrier()
                attn_heads(layer, j, win, WBIG, SCW)
                with sbt("OGT0", [128, KC, 128], BF16) as OGT0, sbt("OGT1", [128, KC, 128], BF16) as OGT1:
                    OGTs = [OGT0, OGT1]

                    def lhs_reads(t):
                        og = OGTs[t % 2]
                        bank = 6 + (t % 2)
                        pt = psb16(bank).rearrange("p (k c) -> p k c", k=KC)
                        cx.group(pe, [lambda k=k: nc.tensor.transpose(
                            pt[:, k, :], SG[:, t, k * 128:(k + 1) * 128], IDENT[:]) for k in range(KC)],
                            reads=[B(f"SG{t}"), B("consts")], writes=[B(f"ps{bank}")])
                        cx.op(dve, lambda: nc.vector.tensor_copy(out=og[:], in_=pt),
                              reads=[B(f"ps{bank}")], writes=[B(f"ogt{t % 2}")])
                        return [B(f"ogt{t % 2}")]

                    out_proj_post(layer, awout_d[j], lambda t, kc: OGTs[t % 2][:, kc, :], lhs_reads, SCW, WOUT=WBIG,
                                  next_layer=next_layer, store=store)
            wb_stack.close()

        def attn_heads(layer, j, win, WBIG, SCW):
            with ExitStack() as es2:
                def sb2(name, shape, dtype):
                    return es2.enter_context(sbt(name, shape, dtype))
                QA = [[sb2(f"QA{m}_{p}", [128, S], BF16) for m in range(2)] for p in range(2)]
                KA = [[sb2(f"KA{m}_{p}", [128, S], BF16) for m in range(2)] for p in range(2)]
                VA = [sb2(f"VA_{p}", [128, NT, 129], BF16) for p in range(2)]
                WS = [[sb2(f"ws{p}_{i}", [128, KC, 128], BF16) for i in range(3)] for p in range(2)]
                NET = 6
                SBK = [0, 1, 4, 5]
                DEPTH = 3
                ET = [sb2(f"et{i}", [128, 2, 256], BF16) for i in range(NET)]
                OS = sb2("OS", [128, 4, 129], F32)
                RZ = sb2("RZ", [128, 4], F32)
                DD = sb2("DD", [128, 2, 128], F32)
                DJ = sb2("DJ", [128, 2, 128], F32)
                SSO = sb2("SSO", [128, 2], F32)
                for p in range(2):
                    cx.op(dve, lambda p=p: nc.vector.memset(QA[p][1][0:64, :], 0.0), writes=[B(f"QA1_{p}")])
                    cx.op(dve, lambda p=p: nc.vector.memset(KA[p][1][0:64, :], 0.0), writes=[B(f"KA1_{p}")])
                    cx.op(dve, lambda p=p: nc.vector.memset(VA[p][:, :, 128:129], 1.0), writes=[B(f"VA_{p}")])

                def load_head_w(h):
                    p = h % 2
                    for i in range(3):
                        c0 = i * D + h * 128
                        load_w(WS[p][i][:], win[:, :, c0:c0 + 128], KC, 128, name=f"ws{p}_{i}")

                def proj_pieces(h):
                    p = h % 2
                    pcs = []
                    pcs.append(lambda: cx.dma(cx.dsem(f"aug{p}"), [
                        (QA[p][0][64:68, :], augq_d[h]), (QA[p][1][0:4, :], augq_d[h]),
                        (KA[p][0][64:68, :], augk_d[h]), (KA[p][1][0:4, :], augk_d[h])],
                        writes=[B(f"QA0_{p}"), B(f"QA1_{p}"), B(f"KA0_{p}"), B(f"KA1_{p}")]))
                    cnt = [0]
                    for wi, tiles, nm in ((0, QA[p], "QA"), (1, KA[p], "KA")):
                        for r in range(4):
                            bank = 6 + (cnt[0] % 2)
                            cnt[0] += 1
                            rd = [B(f"HT{t}") for t in range(4 * r, 4 * r + 4)] + [B(f"ws{p}_{wi}")]

                            def mm(k0, nk=4, r=r, bank=bank, wi=wi, rd=rd):
                                cx.group(pe, [lambda kc=kc: nc.tensor.matmul(
                                    psb(bank), WS[p][wi][:, kc, :], HT[:, kc, r * 512:(r + 1) * 512],
                                    start=(kc == 0), stop=(kc == KC - 1)) for kc in range(k0, k0 + nk)],
                                    reads=rd, writes=[B(f"ps{bank}")])

                            def ev(r=r, bank=bank, tiles=tiles, nm=nm):
                                cx.op(act, lambda: nc.scalar.copy(
                                    out=tiles[0][0:64, r * 512:(r + 1) * 512], in_=PS[0:64, bank, :]),
                                    reads=[B(f"ps{bank}")], writes=[B(f"{nm}0_{p}")])
                                cx.op(dve, lambda: nc.vector.tensor_copy(
                                    out=tiles[1][64:128, r * 512:(r + 1) * 512], in_=PS[64:128, bank, :]),
                                    reads=[B(f"ps{bank}")], writes=[B(f"{nm}1_{p}")])
                            if SPLIT_K:
                                pcs.append(lambda mm=mm: mm(0))
                                pcs.append(lambda mm=mm, ev=ev: (mm(4), ev()))
                            else:
                                pcs.append(lambda mm=mm, ev=ev: (mm(0, 8), ev()))
                    for g in range(4):
                        bank = 6 + (cnt[0] % 2)
                        cnt[0] += 1
                        pv = psb(bank).rearrange("p (a c) -> p a c", a=4)
                        for a in range(4):
                            t = 4 * g + a

                            def mmv(a=a, t=t, pv=pv, bank=bank):
                                cx.group(pe, [lambda kc=kc: nc.tensor.matmul(
                                    pv[:, a, :], HT[:, kc, t * 128:(t + 1) * 128], WS[p][2][:, kc, :],
                                    start=(kc == 0), stop=(kc == KC - 1), skip_group_check=True)
                                    for kc in range(KC)],
                                    reads=[B(f"HT{t}"), B(f"ws{p}_2")], writes=[B(f"ps{bank}")])

                            def evv(g=g, pv=pv, bank=bank):
                                cx.op(dve, lambda: nc.vector.tensor_copy(
                                    out=VA[p][:, 4 * g:4 * g + 4, 0:128], in_=pv),
                                    reads=[B(f"ps{bank}")], writes=[B(f"VA_{p}")])
                            if a < 3:
                                pcs.append(mmv)
                            else:
                                pcs.append(lambda mmv=mmv, evv=evv: (mmv(), evv()))
                    return pcs

                def normalise(h, Q):
                    cx.op(dve, lambda: nc.vector.reciprocal(out=RZ[:], in_=OS[:, :, 128]),
                          reads=[B("OS")], writes=[B("RZ")])
                    cx.op(dve, lambda: nc.vector.tensor_scalar(
                        out=RZ[:, 2:4], in0=RZ[:, 2:4], scalar1=NEGLAM[:, j:j + 1], scalar2=None, op0=ALU.mult),
                        reads=[B("RZ")], writes=[B("RZ")])
                    for qs in range(2):
                        cx.op(dve, lambda qs=qs: nc.vector.tensor_scalar(
                            out=DD[:, qs, :], in0=OS[:, qs, 0:128], scalar1=RZ[:, qs:qs + 1], scalar2=None,
                            op0=ALU.mult), reads=[B("OS"), B("RZ")], writes=[B("DD")])
                        cx.op(dve, lambda qs=qs: nc.vector.scalar_tensor_tensor(
                            out=DD[:, qs, :], in0=OS[:, 2 + qs, 0:128], scalar=RZ[:, 2 + qs:3 + qs],
                            in1=DD[:, qs, :], op0=ALU.mult, op1=ALU.add),
                            reads=[B("OS"), B("RZ"), B("DD")], writes=[B("DD")])
                    cx.op(dve, lambda: nc.vector.tensor_tensor(out=DJ[:], in0=DD[:], in1=DD[:], op=ALU.mult),
                          reads=[B("DD")], writes=[B("DJ")])
                    cx.op(dve, lambda: nc.vector.tensor_reduce(
                        out=SSO[:], in_=DJ[:], axis=mybir.AxisListType.X, op=ALU.add),
                        reads=[B("DJ")], writes=[B("SSO")])
                    cx.op(act, lambda: nc.scalar.activation(
                        out=SSO[:], in_=SSO[:], func=AF.Ln, bias=EPSC[:, 0:1], scale=1.0 / 128),
                        reads=[B("SSO")], writes=[B("SSO")])
                    cx.op(act, lambda: nc.scalar.activation(out=SSO[:], in_=SSO[:], func=AF.Exp, scale=-0.5),
                          reads=[B("SSO")], writes=[B("SSO")])
                    for qs in range(2):
                        t = 2 * Q + qs
                        cx.op(dve, lambda qs=qs, t=t: nc.vector.scalar_tensor_tensor(
                            out=SG[:, t, h * 128:(h + 1) * 128], in0=DD[:, qs, :], scalar=SSO[:, qs:qs + 1],
                            in1=SG[:, t, h * 128:(h + 1) * 128], op0=ALU.mult, op1=ALU.mult),
                            reads=[B("DD"), B("SSO"), B(f"SG{t}")], writes=[B(f"SG{t}")])

                pending = []
                load_head_w(0)
                load_head_w(1)
                for pc in proj_pieces(0):
                    pc()
                bg = BgLoader()
                wv_out = awout_d[j].rearrange("(kc p) c -> p kc c", p=128)
                wout_jobs = [kc for kc in range(KC)]
                for hh in range(2, H):
                    for i in range(3):
                        c0 = i * D + hh * 128
                        bg.add(WS[hh % 2][i][:], win[:, :, c0:c0 + 128], KC, 128, name=f"ws{hh % 2}_{i}", tag=hh)
                    for _ in range(2):
                        if wout_jobs:
                            kc = wout_jobs.pop(0)
                            bg.add(WBIG[:, kc:kc + 1, :], wv_out[:, kc:kc + 1, :], 1, D, scale=SCW[:, 0:1],
                                   name=f"wout{kc}", tag="wout")
                stream = []
                for h in range(H):
                    for Q in range(8):
                        nkt = 2 * Q + 2
                        for kt in range(nkt):
                            jd = kt - 2 * Q
                            c0 = 128 * jd if jd >= 0 else 0
                            stream.append((h, Q, kt, jd, c0, kt, nkt))
                NS = len(stream)

                def emit_S(n):
                    h, Q, kt, jd, c0, i, nkt = stream[n]
                    p = h % 2
                    QAs, KAs = QA[p], KA[p]
                    hb = [B(f"QA0_{p}"), B(f"QA1_{p}"), B(f"KA0_{p}"), B(f"KA1_{p}")]
                    sbank = SBK[n % 4]
                    ps_s = psb(sbank).rearrange("p (m c) -> p m c", m=2)
                    fns = []
                    for m in range(2):
                        kk = 68 if m == 0 else 128
                        fns.append(lambda m=m, kk=kk: nc.tensor.matmul(
                            ps_s[:, m, c0:256], KAs[m][0:kk, kt * 128:(kt + 1) * 128],
                            QAs[m][0:kk, Q * 256 + c0:(Q + 1) * 256],
                            start=True, stop=(jd < 0), skip_group_check=True))
                        if jd >= 0:
                            fns.append(lambda m=m: nc.tensor.matmul(
                                ps_s[:, m, c0:c0 + 128], MASKNEG[:], IDENT[:],
                                start=False, stop=True, skip_group_check=True))
                    cx.group(pe, fns, reads=hb + [B("consts")], writes=[B(f"ps{sbank}")])
                    et = ET[n % NET]
                    cx.op(act, lambda: nc.scalar.activation(
                        out=et[:, :, c0:256], in_=ps_s[:, :, c0:256], func=AF.Exp, scale=0.125),
                        reads=[B(f"ps{sbank}")], writes=[B(f"et{n % NET}")])

                def emit_PV(n):
                    h, Q, kt, jd, c0, i, nkt = stream[n]
                    p = h % 2
                    et = ET[n % NET]
                    fns = []
                    for m in range(2):
                        for qs in range(2):
                            if jd >= 0 and qs < jd:
                                continue
                            last = 2 * Q + qs
                            fns.append(lambda m=m, qs=qs, last=last: nc.tensor.matmul(
                                PS[:, 2 + m, qs * 129:(qs + 1) * 129], et[:, m, qs * 128:(qs + 1) * 128],
                                VA[p][:, kt, :], start=(kt == 0 and qs == 0), stop=(kt == last),
                                skip_group_check=True))
                    cx.group(pe, fns, reads=[B(f"et{n % NET}"), B(f"VA_{p}")], writes=[B("psO")])

                nxt = []
                nxt_head = [0]

                def ensure_proj(h):
                    while nxt:
                        nxt.pop(0)()
                    assert nxt_head[0] >= h

                for n in range(min(DEPTH, NS)):
                    emit_S(n)
                tile_no = 0
                for n in range(NS):
                    h, Q, kt, jd, c0, i, nkt = stream[n]
                    if i == 0 and Q == 0:
                        tile_no = 0
                        bg.cur = h
                        if h + 1 < H:
                            nxt.extend(proj_pieces(h + 1))
                            nxt_head[0] = h + 1
                    if n + DEPTH < NS:
                        h2 = stream[n + DEPTH][0]
                        if h2 != h:
                            ensure_proj(h2)
                        emit_S(n + DEPTH)
                    emit_PV(n)
                    if i == min(3, nkt - 1) and pending:
                        pending.pop()()
                    tile_no += 1
                    if INTERLEAVE and nxt and tile_no % 2 == 0:
                        nxt.pop(0)()
                    if tile_no % 8 == 5:
                        bg.step()
                    if i == nkt - 1:
                        cx.op(dve, lambda: nc.vector.tensor_copy(
                            out=OS[:].rearrange("p a c -> p (a c)").rearrange("p (m x) -> p m x", m=2),
                            in_=PS[:, 2:4, 0:258]), reads=[B("psO")], writes=[B("OS")])
                        pending.append(lambda h=h, Q=Q: normalise(h, Q))
                        if Q == 7:
                            while nxt:
                                nxt.pop(0)()
                            if h + 2 < H:
                                bg.flush(tag=h + 2)
                pending.pop()()
                bg.flush()
                cx.barrier()

        def lru_layer(layer, prenormed=False, next_layer=None, store=None):
            j = layer // 2
            win = lwin_d[j].rearrange("(kc p) c -> p kc c", p=128)
            if not prenormed:
                prenorm(layer)
            with ExitStack() as es2:
                def sb2(name, shape, dtype):
                    return es2.enter_context(sbt(name, shape, dtype))
                WXs = [sb2(f"WX{i}", [128, KC, 256], BF16) for i in range(2)]
                WGT = sb2("WGT", [128, KC, 256], BF16)
                GA = sb2("GA", [128, 2, 256], BF16)
                GX = sb2("GX", [128, 2, 256], BF16)
                XR = sb2("XR", [128, 3 + S], F32)
                XC = sb2("XC", [128, 2, S], F32)
                XCB = sb2("XCB", [128, 2, S], BF16)
                TR = [sb2(f"TR{i}", [128, 512], F32) for i in range(2)]
                HS = [sb2(f"HS{i}", [128, 512], F32) for i in range(2)]
                TI = [sb2(f"TI{i}", [128, 512], F32) for i in range(4)]
                TG = [sb2(f"TG{i}", [128, 512], BF16) for i in range(4)]
                AA = [sb2(f"AA{i}", [128, 512], F32) for i in range(4)]
                A2 = [sb2(f"A2{i}", [128, 512], F32) for i in range(4)]
                cx.op(dve, lambda: nc.vector.memset(XR[:, 0:3], 0.0), writes=[B("XRpad")])
                it = 0
                def load_wx(g):
                    for k0 in range(0, KC, 4):
                        load_w(WXs[g % 2][:, k0:k0 + 4, :], win[:, k0:k0 + 4, g * 256:(g + 1) * 256], 4, 256,
                               eng=act, name=f"wx{g % 2}_{k0}")
                load_wx(0)
                for g in range(4):
                    WX = WXs[g % 2]
                    for k0 in range(0, KC, 4):
                        load_w(WGT[:, k0:k0 + 4, :], win[:, k0:k0 + 4, D + g * 256:D + (g + 1) * 256], 4, 256,
                               eng=act, name=f"wgt{k0}")
                    load_w(GA[:], gaw_d[j, g].rearrange("(ic p) o -> p ic o", p=128), 2, 256, eng=act, name="ga")
                    load_w(GX[:], gxw_d[j, g].rearrange("(ic p) o -> p ic o", p=128), 2, 256, eng=act, name="gx")
                    for cc in range(2):
                        c = 2 * g + cc
                        wcol = lambda tap: COLS[:, C_CONVW + (j * 4 + tap) * KC + c:C_CONVW + (j * 4 + tap) * KC + c + 1]
                        bcol = COLS[:, C_CONVB + j * KC + c:C_CONVB + j * KC + c + 1]
                        cast_pending = []
                        for r in range(4):
                            bank = 6 + (r % 2)
                            cx.group(pe, [lambda kc=kc, r=r, bank=bank, cc=cc: nc.tensor.matmul(
                                psb(bank), WX[:, kc, cc * 128:(cc + 1) * 128], HT[:, kc, r * 512:(r + 1) * 512],
                                start=(kc == 0), stop=(kc == KC - 1)) for kc in range(KC)],
                                reads=[B(f"HT{t}") for t in range(4 * r, 4 * r + 4)] + [B(f"wx{g % 2}_0"), B(f"wx{g % 2}_4")],
                                writes=[B(f"ps{bank}")])
                            xo = XC[:, cc, r * 512:(r + 1) * 512]
                            cx.op(act, lambda r=r, bank=bank: nc.scalar.copy(
                                out=XR[:, 3 + r * 512:3 + (r + 1) * 512], in_=psb(bank)),
                                reads=[B(f"ps{bank}")], writes=[B(f"XR{r}")])
                            cx.op(act, lambda bank=bank, xo=xo: nc.scalar.activation(
                                out=xo, in_=psb(bank), func=AF.Identity, bias=bcol, scale=wcol(3)),
                                reads=[B(f"ps{bank}")], writes=[B(f"XC{cc}_{r}")])
                            while cast_pending:
                                cast_pending.pop()()
                            rd = [B(f"XR{r}"), B("XRpad")] + ([B(f"XR{r - 1}")] if r > 0 else [])
                            for tap in range(3):
                                cx.op(dve, lambda r=r, xo=xo, tap=tap: nc.vector.scalar_tensor_tensor(
                                    out=xo, in0=XR[:, r * 512 + tap:(r + 1) * 512 + tap], scalar=wcol(tap), in1=xo,
                                    op0=ALU.mult, op1=ALU.add), reads=rd + [B(f"XC{cc}_{r}")],
                                    writes=[B(f"XC{cc}_{r}")])
                            cast_pending.append(lambda r=r, xo=xo, cc=cc: cx.op(act, lambda: nc.scalar.copy(
                                out=XCB[:, cc, r * 512:(r + 1) * 512], in_=xo),
                                reads=[B(f"XC{cc}_{r}")], writes=[B(f"XCB{cc}_{r}")]))
                        while cast_pending:
                            cast_pending.pop()()
                    if g + 1 < 4:
                        load_wx(g + 1)
                    for cc in range(2):
                        c = 2 * g + cc
                        cs = slice(cc * 128, (cc + 1) * 128)
                        for r in range(4):
                            q = it % 2
                            it += 1
                            rs = slice(r * 512, (r + 1) * 512)
                            ba, bi, bg = 3 * q, 3 * q + 1, 3 * q + 2
                            xcb_r = [B(f"XCB0_{r}"), B(f"XCB1_{r}")]
                            cx.group(pe, [lambda ic=ic: nc.tensor.matmul(
                                psb(ba), GA[:, ic, cs], XCB[:, ic, rs], start=(ic == 0), stop=(ic == 1))
                                for ic in range(2)], reads=xcb_r + [B("ga")], writes=[B(f"ps{ba}")])
                            cx.group(pe, [lambda ic=ic: nc.tensor.matmul(
                                psb(bi), GX[:, ic, cs], XCB[:, ic, rs], start=(ic == 0), stop=(ic == 1))
                                for ic in range(2)], reads=xcb_r + [B("gx")], writes=[B(f"ps{bi}")])
                            cx.group(pe, [lambda kc=kc: nc.tensor.matmul(
                                psb(bg), WGT[:, kc, cs], HT[:, kc, rs], start=(kc == 0), stop=(kc == KC - 1))
                                for kc in range(KC)],
                                reads=[B(f"HT{t}") for t in range(4 * r, 4 * r + 4)] + [B("wgt0"), B("wgt4")],
                                writes=[B(f"ps{bg}")])
                            cx.op(act, lambda: nc.scalar.activation(
                                out=TR[q][:], in_=psb(ba), func=AF.Tanh, bias=HBS[:, j, c, 0:1], scale=0.5),
                                reads=[B(f"ps{ba}"), B("hbs")], writes=[B(f"TR{q}")])
                            cx.op(act, lambda: nc.scalar.activation(
                                out=TI[r][:], in_=psb(bi), func=AF.Tanh, bias=HBS[:, j, c, 1:2], scale=0.5),
                                reads=[B(f"ps{bi}"), B("hbs")], writes=[B(f"TI{r}")])
                            cx.op(act, lambda: nc.scalar.activation(
                                out=TG[r][:], in_=psb(bg), func=AF.Tanh, scale=0.5),
                                reads=[B(f"ps{bg}")], writes=[B(f"TG{r}")])
                            cx.op(act, lambda: nc.scalar.activation(
                                out=AA[r][:], in_=TR[q][:], func=AF.Exp, bias=CCS[:, j, c, 0:1], scale=CCS[:, j, c, 0:1]),
                                reads=[B(f"TR{q}"), B("ccs")], writes=[B(f"AA{r}")])
                            cx.op(act, lambda: nc.scalar.activation(
                                out=A2[r][:], in_=TR[q][:], func=AF.Exp, bias=CCS[:, j, c, 1:2], scale=CCS[:, j, c, 1:2]),
                                reads=[B(f"TR{q}"), B("ccs")], writes=[B(f"A2{r}")])
                            cx.op(dve, lambda: nc.vector.scalar_tensor_tensor(
                                out=TG[r][:], in0=TG[r][:], scalar=1.0, in1=psb(bg), op0=ALU.add, op1=ALU.mult),
                                reads=[B(f"TG{r}"), B(f"ps{bg}")], writes=[B(f"TG{r}")])
                        for r in range(4):
                            cx.op(act, lambda: nc.scalar.activation(
                                out=A2[r][:], in_=A2[r][:], func=AF.Sqrt, bias=ONEC[:, 0:1], scale=-1.0),
                                reads=[B(f"A2{r}")], writes=[B(f"A2{r}")])
                        for r in range(4):
                            q = r % 2
                            rs = slice(r * 512, (r + 1) * 512)
                            cx.op(dve, lambda: nc.vector.scalar_tensor_tensor(
                                out=TI[r][:], in0=TI[r][:], scalar=1.0, in1=A2[r][:], op0=ALU.add, op1=ALU.mult),
                                reads=[B(f"TI{r}"), B(f"A2{r}")], writes=[B(f"TI{r}")])
                            cx.op(dve, lambda: nc.vector.tensor_tensor(
                                out=TI[r][:], in0=TI[r][:], in1=XC[:, cc, rs], op=ALU.mult),
                                reads=[B(f"TI{r}"), B(f"XC{cc}_{r}")], writes=[B(f"TI{r}")])
                            init = 0.0 if r == 0 else HS[1 - q][:, 511:512]
                            cx.op(dve, lambda init=init: nc.vector.tensor_tensor_scan(
                                out=HS[q][:], data0=AA[r][:], data1=TI[r][:], initial=init,
                                op0=ALU.mult, op1=ALU.add),
                                reads=[B(f"AA{r}"), B(f"TI{r}")] + ([B(f"HS{1 - q}")] if r > 0 else []),
                                writes=[B(f"HS{q}")])
                            cx.op(dve, lambda: nc.vector.scalar_tensor_tensor(
                                out=YGT[:, c, rs], in0=HS[q][:], scalar=0.25, in1=TG[r][:], op0=ALU.mult, op1=ALU.mult),
                                reads=[B(f"HS{q}"), B(f"TG{r}")], writes=[B(f"YG{c}")])
                cx.barrier()
            wb = None
            hook = None
            if next_layer is not None and next_layer % 2 == 0:
                wb = alloc_wbig()
                hook = lambda: load_gate_w(next_layer, wb[1])
            out_proj_post(layer, lwout_d[j], lambda t, kc: YGT[:, kc, t * 128:(t + 1) * 128],
                          lambda t: [B(f"YG{c}") for c in range(KC)], None, weng=act,
                          next_layer=next_layer, store=store, pre_hook=hook)
            return wb

        EPSC = sb("EPSC", [128, 1], F32)
        cx.op(dve, lambda: nc.vector.memset(EPSC[:], EPS), writes=[B("epsc")])
        cx.op(dve, lambda: nc.vector.memset(ONEC[:], 1.0), writes=[B("onec")])
        cx.barrier()
        for jj in range(2):
            la = COLS[:, C_LOGA + jj * KC:C_LOGA + (jj + 1) * KC]
            cx.op(act, lambda: nc.scalar.activation(out=CCS[:, jj, :, 0], in_=la, func=AF.Exp, scale=-1.0),
                  writes=[B("ccs")])
            cx.op(act, lambda: nc.scalar.activation(out=CCS[:, jj, :, 1], in_=CCS[:, jj, :, 0], func=AF.Ln,
                                                    bias=ONEC[:, 0:1], scale=1.0),
                  reads=[B("ccs")], writes=[B("ccs")])
            cx.op(dve, lambda: nc.vector.tensor_scalar(out=CCS[:, jj, :, 0], in0=CCS[:, jj, :, 1], scalar1=-4.0,
                                                       scalar2=None, op0=ALU.mult),
                  reads=[B("ccs")], writes=[B("ccs")])
            cx.op(dve, lambda: nc.vector.tensor_scalar(out=CCS[:, jj, :, 1], in0=CCS[:, jj, :, 1], scalar1=-8.0,
                                                       scalar2=None, op0=ALU.mult),
                  reads=[B("ccs")], writes=[B("ccs")])
            cx.op(dve, lambda: nc.vector.tensor_scalar(out=HBS[:, jj, :, 0], in0=COLS[:, C_GAB + jj * KC:C_GAB + (jj + 1) * KC],
                                                       scalar1=0.5, scalar2=None, op0=ALU.mult), writes=[B("hbs")])
            cx.op(dve, lambda: nc.vector.tensor_scalar(out=HBS[:, jj, :, 1], in0=COLS[:, C_GXB + jj * KC:C_GXB + (jj + 1) * KC],
                                                       scalar1=0.5, scalar2=None, op0=ALU.mult), writes=[B("hbs")])
        cx.barrier()
        def xin(s, g):
            xv = x_d[s].rearrange("(t p) d -> p t d", p=128)
            cx.dma(cx.dsem(f"xin{g}"), [(X[:, 4 * g:4 * g + 4, :], xv[:, 4 * g:4 * g + 4, :])],
                   writes=[B(f"X{t}") for t in range(4 * g, 4 * g + 4)])

        for g in range(4):
            xin(0, g)
        for s in range(nseq):
            yv = y_d[s].rearrange("(t p) d -> p t d", p=128)

            def store(t, s=s, yv=yv):
                g = t // 4
                cx.dma(cx.dsem(f"yout{t}"), [(yv[:, t, :], X[:, t, :])], reads=[B(f"X{t}")])
                if t % 4 == 3 and s + 1 < nseq:
                    xin(s + 1, g)

            prenormed = False
            wb = None
            for idx, layer in enumerate(layers):
                nxt = layers[idx + 1] if idx + 1 < len(layers) else None
                st = store if nxt is None else None
                if layer % 2 == 0:
                    attn_layer(layer, prenormed, wb, nxt, st)
                    wb = None
                else:
                    wb = lru_layer(layer, prenormed, nxt, st)
                prenormed = nxt is not None
            cx.barrier()
        for t in range(NT):
            d = cx.dsems[f"yout{t}"]
            nc.sync.wait_ge(d.sem, d.val)
    return nc


def _host_consts():
    ident = np.eye(128, dtype=np.float32).astype(ml_dtypes.bfloat16)
    pos = np.arange(S)
    a128 = (pos // 128 * 128).astype(np.float32)
    b = (pos % 128).astype(np.float32)
    one = np.ones(S, np.float32)
    augq = np.zeros((H, 4, S), np.float32)
    augk = np.zeros((H, 4, S), np.float32)
    for h in range(H):
        s8 = 8.0 * 2.0 ** (-(h + 1))
        augk[h, 0], augq[h, 0] = s8 * one, -a128
        augk[h, 1], augq[h, 1] = s8 * one, -b
        augk[h, 2], augq[h, 2] = a128, s8 * one
        augk[h, 3], augq[h, 3] = b, s8 * one
    return ident, augq.astype(ml_dtypes.bfloat16), augk.astype(ml_dtypes.bfloat16)


def _pack_cols(inp):
    cols = np.zeros((128, C_TOTAL), np.float32)

    def colmajor(v):
        v = np.asarray(v, np.float32)
        lead = v.shape[:-1]
        return np.moveaxis(v.reshape(*lead, KC, 128), -1, 0).reshape(128, -1)

    cols[:, C_PREW:C_PREW + 32] = colmajor(inp["pre_norm_w"])
    cols[:, C_SUBLN:C_SUBLN + 2] = np.asarray(inp["attn_subln_w"], np.float32).T
    cols[:, C_CONVW:C_CONVW + 64] = colmajor(inp["lru_conv_w"])
    cols[:, C_CONVB:C_CONVB + 16] = colmajor(inp["lru_conv_b"])
    cols[:, C_GAB:C_GAB + 16] = colmajor(inp["lru_gate_a_b"])
    cols[:, C_GXB:C_GXB + 16] = colmajor(inp["lru_gate_x_b"])
    cols[:, C_LOGA:C_LOGA + 16] = colmajor(inp["lru_log_a_param"])
    return cols


def make_in_maps(inp, n_cores=N_CORES):
    f = lambda k: np.ascontiguousarray(np.asarray(inp[k], np.float32))
    ident, augq, augk = _host_consts()
    cols = _pack_cols(inp)
    postw = np.ascontiguousarray(np.broadcast_to(f("post_norm_w")[:, None, :], (4, 128, D)))
    prewr = np.ascontiguousarray(np.broadcast_to(f("pre_norm_w")[:, None, :], (4, 128, D)))
    kk = np.arange(128)
    tri1 = (kk[None, :] >= kk[:, None]).astype(np.float32)
    tri = np.ascontiguousarray(np.broadcast_to(tri1[:, None, :], (128, 2, 128))).astype(ml_dtypes.bfloat16)
    maskneg = np.where(kk[None, :] > kk[:, None], -30000.0, 0.0).astype(np.float32).astype(ml_dtypes.bfloat16)
    lam = np.stack([f("attn_lambda_q1"), f("attn_lambda_k1"), f("attn_lambda_q2"), f("attn_lambda_k2")], axis=1)
    lamrows = np.ascontiguousarray(np.broadcast_to(lam[None], (128, 2, 4, 64)))
    x = f("x")
    shared = {
        "attn_w_in": f("attn_w_in"), "attn_w_out": f("attn_w_out"),
        "lru_w_in": f("lru_w_in"), "lru_w_out": f("lru_w_out"),
        "lru_gate_a_w": f("lru_gate_a_w"), "lru_gate_x_w": f("lru_gate_x_w"),
        "cols": cols, "postw": postw, "prewr": prewr, "tri": tri, "maskneg": maskneg, "lamrows": lamrows,
        "ident": ident, "augq": augq, "augk": augk,
    }
    maps = []
    for c in range(n_cores):
        m = dict(shared)
        m["x"] = np.ascontiguousarray(x[c * SEQ_PER_CORE:(c + 1) * SEQ_PER_CORE])
        maps.append(m)
    return maps


def kernel(**inputs):
    nc = build_nc()
    in_maps = make_in_maps(inputs)
    res = run_bass_kernel_spmd(nc, in_maps, core_ids=list(range(N_CORES)))
    return np.concatenate([np.asarray(r["y"], np.float32) for r in res.results], axis=0)
```

```python
import math
from contextlib import ExitStack

import numpy as np
import ml_dtypes
import concourse.bass as bass
import concourse.mybir as mybir
from concourse.alu_op_type import AluOpType as ALU
from concourse.bass_utils import run_bass_kernel_spmd

F32 = mybir.dt.float32
BF16 = mybir.dt.bfloat16
AF = mybir.ActivationFunctionType

S = 2048
D = 1024
NT = 16
KC = 8
H = 8
EPS = 1e-6
N_CORES = 8
SEQ_PER_CORE = 2
DMA_SCRATCH = 512
INTERLEAVE = True
SPLIT_K = False

C_PREW = 0
C_SUBLN = 32
C_CONVW = 34
C_CONVB = 98
C_GAB = 114
C_GXB = 130
C_LOGA = 146
C_TOTAL = 162


class Ev:
    __slots__ = ("sem", "key", "val", "eng")

    def __init__(self, sem, key, val, eng):
        self.sem, self.key, self.val, self.eng = sem, key, val, eng


class Buf:
    __slots__ = ("name", "w", "r")

    def __init__(self, name):
        self.name = name
        self.w = None
        self.r = {}


class Eng:
    def __init__(self, name, h, sem):
        self.name, self.h, self.sem = name, h, sem
        self.cnt = 0
        self.waited = {}


class DSem:
    def __init__(self, name, sem):
        self.name, self.sem, self.val = name, sem, 0


class Ctx:
    def __init__(self, nc, es):
        self.nc = nc
        self.es = es
        self.pe = Eng("pe", nc.tensor, es.enter_context(nc.semaphore("s_pe")))
        self.act = Eng("act", nc.scalar, es.enter_context(nc.semaphore("s_act")))
        self.dve = Eng("dve", nc.vector, es.enter_context(nc.semaphore("s_dve")))
        self.pool = Eng("pool", nc.gpsimd, es.enter_context(nc.semaphore("s_pool")))
        self.sp = Eng("sp", nc.sync, es.enter_context(nc.semaphore("s_sp")))
        self.engs = [self.pe, self.act, self.dve, self.pool, self.sp]
        self.dsems = {}
        self.bufs = {}

    def buf(self, name):
        b = self.bufs.get(name)
        if b is None:
            b = self.bufs[name] = Buf(name)
        return b

    def dsem(self, name):
        d = self.dsems.get(name)
        if d is None:
            d = self.dsems[name] = DSem(name, self.es.enter_context(self.nc.semaphore("d_" + name)))
        return d

    def _wait(self, eng, ev):
        if ev is None:
            return
        if ev.eng == "pe" and eng.name == "pe":
            return
        if eng.waited.get(ev.key, 0) >= ev.val:
            return
        eng.h.wait_ge(ev.sem, ev.val)
        eng.waited[ev.key] = ev.val

    def _deps(self, eng, reads, writes):
        for b in reads:
            self._wait(eng, b.w)
        for b in writes:
            self._wait(eng, b.w)
            for ev in b.r.values():
                self._wait(eng, ev)

    def _mark(self, ev, rkey, reads, writes):
        for b in reads:
            b.r[rkey] = ev
        for b in writes:
            b.w = ev
            b.r = {}

    def op(self, eng, fn, reads=(), writes=()):
        self._deps(eng, reads, writes)
        inst = fn()
        eng.cnt += 1
        inst.then_inc(eng.sem, 1)
        ev = Ev(eng.sem, eng.name, eng.cnt, eng.name)
        eng.waited[eng.name] = max(eng.waited.get(eng.name, 0), 0)
        self._mark(ev, eng.name, reads, writes)
        return ev

    def group(self, eng, fns, reads=(), writes=()):
        self._deps(eng, reads, writes)
        inst = None
        for fn in fns:
            inst = fn()
        eng.cnt += 1
        inst.then_inc(eng.sem, 1)
        ev = Ev(eng.sem, eng.name, eng.cnt, eng.name)
        self._mark(ev, eng.name, reads, writes)
        return ev

    def dma(self, dsem, pairs, reads=(), writes=()):
        q = self.sp
        self._deps(q, reads, writes)
        for o, i in pairs:
            q.h.dma_start(out=o, in_=i).then_inc(dsem.sem, 16)
            dsem.val += 16
        ev = Ev(dsem.sem, "d_" + dsem.name, dsem.val, "dma")
        self._mark(ev, "d_" + dsem.name, reads, writes)
        return ev

    def barrier(self):
        evs = []
        for e in self.engs:
            if e.cnt > 0:
                evs.append(Ev(e.sem, e.name, e.cnt, "x"))
        for d in self.dsems.values():
            if d.val > 0:
                evs.append(Ev(d.sem, "d_" + d.name, d.val, "dma"))
        for e in self.engs:
            for ev in evs:
                if e.waited.get(ev.key, 0) >= ev.val:
                    continue
                if ev.key == e.name and e.name == "sp":
                    continue
                e.h.wait_ge(ev.sem, ev.val)
                e.waited[ev.key] = ev.val
        for b in self.bufs.values():
            b.w = None
            b.r = {}


def build_nc(layers=(0, 1, 2, 3), nseq=SEQ_PER_CORE):
    nc = bass.Bass("TRN2", target_bir_lowering=False, dynamic_dma_scratch_size=DMA_SCRATCH)
    dt = nc.dram_tensor
    x_d = dt("x", [SEQ_PER_CORE, S, D], F32, kind="ExternalInput").ap()
    y_d = dt("y", [SEQ_PER_CORE, S, D], F32, kind="ExternalOutput").ap()
    awin_d = dt("attn_w_in", [2, D, 4 * D], F32, kind="ExternalInput").ap()
    awout_d = dt("attn_w_out", [2, D, D], F32, kind="ExternalInput").ap()
    lwin_d = dt("lru_w_in", [2, D, 2 * D], F32, kind="ExternalInput").ap()
    lwout_d = dt("lru_w_out", [2, D, D], F32, kind="ExternalInput").ap()
    gaw_d = dt("lru_gate_a_w", [2, 4, 256, 256], F32, kind="ExternalInput").ap()
    gxw_d = dt("lru_gate_x_w", [2, 4, 256, 256], F32, kind="ExternalInput").ap()
    cols_d = dt("cols", [128, C_TOTAL], F32, kind="ExternalInput").ap()
    postw_d = dt("postw", [4, 128, D], F32, kind="ExternalInput").ap()
    lamrows_d = dt("lamrows", [128, 2, 4, 64], F32, kind="ExternalInput").ap()
    prewr_d = dt("prewr", [4, 128, D], F32, kind="ExternalInput").ap()
    tri_d = dt("tri", [128, 2, 128], BF16, kind="ExternalInput").ap()
    maskneg_d = dt("maskneg", [128, 128], BF16, kind="ExternalInput").ap()
    ident_d = dt("ident", [128, 128], BF16, kind="ExternalInput").ap()
    augq_d = dt("augq", [H, 4, S], BF16, kind="ExternalInput").ap()
    augk_d = dt("augk", [H, 4, S], BF16, kind="ExternalInput").ap()

    with ExitStack() as es:
        cx = Ctx(nc, es)
        pe, act, dve, pool = cx.pe, cx.act, cx.dve, cx.pool
        B = cx.buf

        uniq = [0]

        def sbt(name, shape, dtype):
            uniq[0] += 1
            return nc.sbuf_tensor(f"{name}_{uniq[0]}", shape, dtype)

        def sb(name, shape, dtype):
            return es.enter_context(sbt(name, shape, dtype))

        X = sb("X", [128, NT, D], F32)
        HT = sb("HT", [128, KC, S], BF16)
        SG = sb("SG", [128, NT, D], BF16)
        YGT = SG[:].rearrange("p t d -> p (t d)").rearrange("p (c s) -> p c s", c=KC)
        STG = [sb(f"stg{i}", [128, 1024], F32) for i in range(2)]
        COLS = sb("COLS", [128, C_TOTAL], F32)
        IDENT = sb("IDENT", [128, 128], BF16)
        TRI = sb("TRI", [128, 2, 128], BF16)
        MASKNEG = sb("MASKNEG", [128, 128], BF16)
        NEGLAM = sb("NEGLAM", [128, 2], F32)
        SUBC = sb("SUBC", [128, 4], F32)
        SS = sb("SS", [128, NT], F32)
        RSTD = sb("RSTD", [128, NT], F32)
        LNV = sb("LNV", [128, NT], F32)
        SS2 = sb("SS2", [128, 4], F32)
        CCS = sb("CCS", [128, 2, KC, 2], F32)
        HBS = sb("HBS", [128, 2, KC, 2], F32)
        ONEC = sb("ONEC", [128, 1], F32)
        PS = es.enter_context(nc.psum_tensor("PS", [128, 8, 512], F32))

        def psb(b):
            return PS[:, b, :]

        def psb16(b):
            return PS[:, b, :].bitcast(BF16)

        stg_i = [0]

        def load_w(dst, src, n, W, eng=None, scale=None, name="w"):
            eng = eng or dve
            i = stg_i[0] % 2
            stg_i[0] += 1
            st = STG[i][:, 0:n * W].rearrange("p (n w) -> p n w", n=n)
            cx.dma(cx.dsem(f"stg{i}"), [(st, src)], writes=[B(f"stg{i}")])
            if eng is dve:
                if scale is None:
                    fn = lambda: nc.vector.tensor_copy(out=dst, in_=st)
                else:
                    fn = lambda: nc.vector.tensor_scalar(out=dst, in0=st, scalar1=scale, scalar2=None, op0=ALU.mult)
            else:
                if scale is None:
                    fn = lambda: nc.scalar.copy(out=dst, in_=st)
                else:
                    fn = lambda: nc.scalar.activation(out=dst, in_=st, func=AF.Copy, scale=scale)
            cx.op(eng, fn, reads=[B(f"stg{i}")] + ([B("scw")] if scale is not None else []), writes=[B(name)])

        cx.dma(cx.dsem("c0"), [(COLS[:], cols_d), (IDENT[:], ident_d), (TRI[:], tri_d), (MASKNEG[:], maskneg_d)],
               writes=[B("consts")])
        with sbt("LAMR", [128, 2, 4, 64], F32) as LAMR, \
                sbt("LAMS", [128, 4], F32) as LAMS, \
                sbt("LAMJ", [128, 64], F32) as LAMJ:
            cx.dma(cx.dsem("c1"), [(LAMR[:], lamrows_d)], writes=[B("lamr")])
            for j in range(2):
                layer = 2 * j
                lam_init = 0.8 - 0.6 * math.exp(-0.3 * layer)
                for q in range(2):
                    cx.op(dve, lambda j=j, q=q: nc.vector.tensor_tensor(
                        out=LAMJ[:], in0=LAMR[:, j, 2 * q, :], in1=LAMR[:, j, 2 * q + 1, :], op=ALU.mult),
                        reads=[B("lamr")], writes=[B("lamj")])
                    cx.op(dve, lambda j=j, q=q: nc.vector.tensor_reduce(
                        out=LAMS[:, 2 * j + q:2 * j + q + 1], in_=LAMJ[:], axis=mybir.AxisListType.X, op=ALU.add),
                        reads=[B("lamj")], writes=[B("lams")])
                cx.op(act, lambda j=j: nc.scalar.activation(
                    out=LAMS[:, 2 * j:2 * j + 2], in_=LAMS[:, 2 * j:2 * j + 2], func=AF.Exp),
                    reads=[B("lams")], writes=[B("lams")])
                cx.op(dve, lambda j=j: nc.vector.tensor_tensor(
                    out=NEGLAM[:, j:j + 1], in0=LAMS[:, 2 * j + 1:2 * j + 2], in1=LAMS[:, 2 * j:2 * j + 1],
                    op=ALU.subtract), reads=[B("lams")], writes=[B("neglam")])
                cx.op(dve, lambda j=j, li=lam_init: nc.vector.tensor_scalar(
                    out=NEGLAM[:, j:j + 1], in0=NEGLAM[:, j:j + 1], scalar1=-li, scalar2=None, op0=ALU.add),
                    reads=[B("neglam")], writes=[B("neglam")])
                cx.op(dve, lambda j=j, li=lam_init: nc.vector.tensor_scalar(
                    out=SUBC[:, j:j + 1], in0=COLS[:, C_SUBLN + j:C_SUBLN + j + 1], scalar1=1.0 - li,
                    scalar2=None, op0=ALU.mult), reads=[B("consts")], writes=[B("subc")])
            cx.barrier()

        def prenorm_ops(HN, JUNK, PREWR):
            def a(t):
                cx.op(act, lambda: nc.scalar.activation(
                    out=JUNK[:], in_=X[:, t, :], func=AF.Square, accum_out=SS[:, t:t + 1]),
                    reads=[B(f"X{t}")], writes=[B("junk"), B(f"ss{t}")])
                cx.op(act, lambda: nc.scalar.activation(
                    out=LNV[:, t:t + 1], in_=SS[:, t:t + 1], func=AF.Ln, bias=EPSC[:, 0:1], scale=1.0 / D),
                    reads=[B(f"ss{t}")], writes=[B(f"lnv{t}")])
                cx.op(act, lambda: nc.scalar.activation(
                    out=RSTD[:, t:t + 1], in_=LNV[:, t:t + 1], func=AF.Exp, scale=-0.5),
                    reads=[B(f"lnv{t}")], writes=[B(f"rstd{t}")])

            def b(t):
                hn = HN[t % 2]
                cx.op(dve, lambda: nc.vector.scalar_tensor_tensor(
                    out=hn[:], in0=X[:, t, :], scalar=RSTD[:, t:t + 1], in1=PREWR[:], op0=ALU.mult, op1=ALU.mult),
                    reads=[B(f"X{t}"), B(f"rstd{t}"), B("prewr")], writes=[B(f"hn{t % 2}")])
                bank = 4 + (t % 2)
                pt = psb16(bank).rearrange("p (k c) -> p k c", k=KC)
                cx.group(pe, [lambda k=k: nc.tensor.transpose(
                    pt[:, k, :], hn[:, k * 128:(k + 1) * 128], IDENT[:]) for k in range(KC)],
                    reads=[B(f"hn{t % 2}"), B("consts")], writes=[B(f"ps{bank}")])

            def c(t):
                bank = 4 + (t % 2)
                pt = psb16(bank).rearrange("p (k c) -> p k c", k=KC)
                cx.op(dve, lambda: nc.vector.tensor_copy(out=HT[:, :, t * 128:(t + 1) * 128], in_=pt),
                      reads=[B(f"ps{bank}")], writes=[B(f"HT{t}")])
            return a, b, c

        def prenorm(layer):
            with sbt("hn0", [128, D], BF16) as HN0, sbt("hn1", [128, D], BF16) as HN1, \
                    sbt("JUNK", [128, D], BF16) as JUNK, sbt("PREWR", [128, D], F32) as PREWR:
                cx.dma(cx.dsem("prewr"), [(PREWR[:], prewr_d[layer])], writes=[B("prewr")])
                a, b, c = prenorm_ops([HN0, HN1], JUNK, PREWR)
                for t in range(NT):
                    a(t)
                b(0)
                for t in range(NT):
                    if t + 1 < NT:
                        b(t + 1)
                    c(t)
                cx.barrier()

        def load_wout(WOUT, wout_d, scale_cols, weng):
            wv = wout_d.rearrange("(kc p) c -> p kc c", p=128)
            for kc in range(KC):
                load_w(WOUT[:, kc:kc + 1, :], wv[:, kc:kc + 1, :], 1, D, eng=weng,
                       scale=None if scale_cols is None else scale_cols[:, 0:1], name=f"wout{kc}")

        def out_proj_post(layer, wout_d, lhs_fn, lhs_reads_fn, scale_cols, weng=None, WOUT=None,
                          next_layer=None, store=None, pre_hook=None):
            with ExitStack() as es3:
                POSTW = es3.enter_context(sbt("POSTW", [128, D], F32))
                JUNK = es3.enter_context(sbt("JUNK2", [128, D], BF16))
                TMPY = [es3.enter_context(sbt(f"tmpy{i}", [128, D], F32)) for i in range(2)]
                pa = pb = pc = None
                if next_layer is not None:
                    HN = [es3.enter_context(sbt(f"hn{i}", [128, D], BF16)) for i in range(2)]
                    PREWR = es3.enter_context(sbt("PREWR", [128, D], F32))
                    pa, pb, pc = prenorm_ops(HN, JUNK, PREWR)
                if WOUT is None:
                    WOUT = es3.enter_context(sbt("WOUT", [128, KC, D], BF16))
                    XSTG = [es3.enter_context(sbt(f"xstg{i}", [128, D], F32)) for i in range(4)]
                    wv = wout_d.rearrange("(kc p) c -> p kc c", p=128)
                    for kc in range(KC):
                        st = XSTG[kc % 4]
                        cx.dma(cx.dsem(f"xstg{kc % 4}"), [(st[:], wv[:, kc, :])], writes=[B(f"xstg{kc % 4}")])
                        if kc % 2 == 0:
                            cx.op(act, lambda kc=kc, st=st: nc.scalar.copy(out=WOUT[:, kc, :], in_=st[:]),
                                  reads=[B(f"xstg{kc % 4}")], writes=[B(f"wout{kc}")])
                        else:
                            cx.op(dve, lambda kc=kc, st=st: nc.vector.tensor_copy(out=WOUT[:, kc, :], in_=st[:]),
                                  reads=[B(f"xstg{kc % 4}")], writes=[B(f"wout{kc}")])
                cx.dma(cx.dsem("postw"), [(POSTW[:], postw_d[layer])], writes=[B("postw")])
                if next_layer is not None:
                    cx.dma(cx.dsem("prewr"), [(PREWR[:], prewr_d[next_layer])], writes=[B("prewr")])
                if pre_hook is not None:
                    pre_hook()
                nxt_reads = lhs_reads_fn(0)
                for t in range(NT):
                    reads = nxt_reads
                    b0 = 2 * (t % 2)
                    for half in range(2):
                        bank = b0 + half
                        cx.group(pe, [lambda kc=kc, t=t, half=half, bank=bank: nc.tensor.matmul(
                            psb(bank), lhs_fn(t, kc), WOUT[:, kc, half * 512:(half + 1) * 512],
                            start=(kc == 0), stop=(kc == KC - 1)) for kc in range(KC)],
                            reads=reads + [B(f"wout{kc}") for kc in range(KC)], writes=[B(f"ps{bank}")])
                    if pb is not None:
                        if t >= 2:
                            pb(t - 2)
                        if t >= 3:
                            pc(t - 3)
                    if t + 1 < NT:
                        nxt_reads = lhs_reads_fn(t + 1)
                    py = PS[:, b0:b0 + 2, :].rearrange("p b c -> p (b c)")
                    pyb = [B(f"ps{b0}"), B(f"ps{b0 + 1}")]
                    c2 = t % 4
                    cx.op(act, lambda py=py, c2=c2: nc.scalar.activation(
                        out=JUNK[:], in_=py, func=AF.Square, accum_out=SS2[:, c2:c2 + 1]),
                        reads=pyb, writes=[B("junk"), B(f"ss2_{c2}")])
                    cx.op(act, lambda c2=c2: nc.scalar.activation(
                        out=SS2[:, c2:c2 + 1], in_=SS2[:, c2:c2 + 1], func=AF.Ln, bias=EPSC[:, 0:1], scale=1.0 / D),
                        reads=[B(f"ss2_{c2}")], writes=[B(f"ss2_{c2}")])
                    cx.op(act, lambda c2=c2: nc.scalar.activation(
                        out=SS2[:, c2:c2 + 1], in_=SS2[:, c2:c2 + 1], func=AF.Exp, scale=-0.5),
                        reads=[B(f"ss2_{c2}")], writes=[B(f"ss2_{c2}")])
                    if pa is not None and t >= 1:
                        pa(t - 1)
                    ty = TMPY[t % 2]
                    cx.op(dve, lambda py=py, ty=ty, c2=c2: nc.vector.scalar_tensor_tensor(
                        out=ty[:], in0=py, scalar=SS2[:, c2:c2 + 1], in1=POSTW[:], op0=ALU.mult, op1=ALU.mult),
                        reads=pyb + [B(f"ss2_{c2}"), B("postw")], writes=[B(f"tmpy{t % 2}")])
                    cx.op(dve, lambda t=t, ty=ty: nc.vector.tensor_tensor(
                        out=X[:, t, :], in0=X[:, t, :], in1=ty[:], op=ALU.add),
                        reads=[B(f"tmpy{t % 2}")], writes=[B(f"X{t}")])
                    if store is not None:
                        store(t)
                if pa is not None:
                    pa(NT - 1)
                    pb(NT - 2)
                    pc(NT - 3)
                    pb(NT - 1)
                    pc(NT - 2)
                    pc(NT - 1)
                cx.barrier()

        def alloc_wbig():
            st = ExitStack()
            WBIG = st.enter_context(sbt("WBIG", [128, KC, D], BF16))
            return st, WBIG

        def load_gate_w(layer, WBIG):
            win = awin_d[layer // 2].rearrange("(kc p) c -> p kc c", p=128)
            for half in range(2):
                c0 = 3 * D + half * 512
                for kc in range(0, KC, 2):
                    load_w(WBIG[:, kc:kc + 2, half * 512:(half + 1) * 512], win[:, kc:kc + 2, c0:c0 + 512],
                           2, 512, name=f"wg{half}_{kc}")

        def attn_layer(layer, prenormed=False, wb=None, next_layer=None, store=None):
            j = layer // 2
            win = awin_d[j].rearrange("(kc p) c -> p kc c", p=128)
            if wb is None:
                wb = alloc_wbig()
                load_gate_w(layer, wb[0 + 1])
            wb_stack, WBIG = wb
            with ExitStack() as esl:
                SCW = esl.enter_context(sbt("SCW", [128, 1], F32))
                cx.op(dve, lambda: nc.vector.tensor_copy(out=SCW[:], in_=SUBC[:, j:j + 1]), writes=[B("scw")])
                if not prenormed:
                    prenorm(layer)
                else:
                    cx.barrier()
                for half in range(2):
                    for t in range(NT):
                        bank = 6 + (t % 2)
                        cx.group(pe, [lambda kc=kc, t=t, bank=bank, half=half: nc.tensor.matmul(
                            psb(bank), HT[:, kc, t * 128:(t + 1) * 128], WBIG[:, kc, half * 512:(half + 1) * 512],
                            start=(kc == 0), stop=(kc == KC - 1)) for kc in range(KC)],
                            reads=[B(f"HT{t}")], writes=[B(f"ps{bank}")])
                        cx.op(act, lambda t=t, half=half, bank=bank: nc.scalar.activation(
                            out=SG[:, t, half * 512:(half + 1) * 512], in_=psb(bank), func=AF.Silu),
                            reads=[B(f"ps{bank}")], writes=[B(f"SG{t}")])
                cx.barrier()
                attn_heads(layer, j, win, WBIG, SCW)
                with sbt("OGT0", [128, KC, 128], BF16) as OGT0, sbt("OGT1", [128, KC, 128], BF16) as OGT1:
                    OGTs = [OGT0, OGT1]

                    def lhs_reads(t):
                        og = OGTs[t % 2]
                        bank = 6 + (t % 2)
                        pt = psb16(bank).rearrange("p (k c) -> p k c", k=KC)
                        cx.group(pe, [lambda k=k: nc.tensor.transpose(
                            pt[:, k, :], SG[:, t, k * 128:(k + 1) * 128], IDENT[:]) for k in range(KC)],
                            reads=[B(f"SG{t}"), B("consts")], writes=[B(f"ps{bank}")])
                        cx.op(dve, lambda: nc.vector.tensor_copy(out=og[:], in_=pt),
                              reads=[B(f"ps{bank}")], writes=[B(f"ogt{t % 2}")])
                        return [B(f"ogt{t % 2}")]

                    out_proj_post(layer, awout_d[j], lambda t, kc: OGTs[t % 2][:, kc, :], lhs_reads, SCW, WOUT=WBIG,
                                  next_layer=next_layer, store=store)
            wb_stack.close()

        def attn_heads(layer, j, win, WBIG, SCW):
            with ExitStack() as es2:
                def sb2(name, shape, dtype):
                    return es2.enter_context(sbt(name, shape, dtype))
                QA = [[sb2(f"QA{m}_{p}", [128, S], BF16) for m in range(2)] for p in range(2)]
                KA = [[sb2(f"KA{m}_{p}", [128, S], BF16) for m in range(2)] for p in range(2)]
                VA = [sb2(f"VA_{p}", [128, NT, 129], BF16) for p in range(2)]
                WS = [[sb2(f"ws{p}_{i}", [128, KC, 128], BF16) for i in range(3)] for p in range(2)]
                NET = 6
                SBK = [0, 1, 4, 5]
                DEPTH = 3
                ET = [sb2(f"et{i}", [128, 2, 256], BF16) for i in range(NET)]
                OS = sb2("OS", [128, 4, 129], F32)
                RZ = sb2("RZ", [128, 4], F32)
                DD = sb2("DD", [128, 2, 128], F32)
                DJ = sb2("DJ", [128, 2, 128], F32)
                SSO = sb2("SSO", [128, 2], F32)
                for p in range(2):
                    cx.op(dve, lambda p=p: nc.vector.memset(QA[p][1][0:64, :], 0.0), writes=[B(f"QA1_{p}")])
                    cx.op(dve, lambda p=p: nc.vector.memset(KA[p][1][0:64, :], 0.0), writes=[B(f"KA1_{p}")])
                    cx.op(dve, lambda p=p: nc.vector.memset(VA[p][:, :, 128:129], 1.0), writes=[B(f"VA_{p}")])

                def load_head_w(h):
                    p = h % 2
                    for i in range(3):
                        c0 = i * D + h * 128
                        load_w(WS[p][i][:], win[:, :, c0:c0 + 128], KC, 128, name=f"ws{p}_{i}")

                def proj_pieces(h):
                    p = h % 2
                    pcs = []
                    pcs.append(lambda: cx.dma(cx.dsem(f"aug{p}"), [
                        (QA[p][0][64:68, :], augq_d[h]), (QA[p][1][0:4, :], augq_d[h]),
                        (KA[p][0][64:68, :], augk_d[h]), (KA[p][1][0:4, :], augk_d[h])],
                        writes=[B(f"QA0_{p}"), B(f"QA1_{p}"), B(f"KA0_{p}"), B(f"KA1_{p}")]))
                    cnt = [0]
                    for wi, tiles, nm in ((0, QA[p], "QA"), (1, KA[p], "KA")):
                        for r in range(4):
                            bank = 6 + (cnt[0] % 2)
                            cnt[0] += 1
                            rd = [B(f"HT{t}") for t in range(4 * r, 4 * r + 4)] + [B(f"ws{p}_{wi}")]

                            def mm(k0, nk=4, r=r, bank=bank, wi=wi, rd=rd):
                                cx.group(pe, [lambda kc=kc: nc.tensor.matmul(
                                    psb(bank), WS[p][wi][:, kc, :], HT[:, kc, r * 512:(r + 1) * 512],
                                    start=(kc == 0), stop=(kc == KC - 1)) for kc in range(k0, k0 + nk)],
                                    reads=rd, writes=[B(f"ps{bank}")])

                            def ev(r=r, bank=bank, tiles=tiles, nm=nm):
                                cx.op(act, lambda: nc.scalar.copy(
                                    out=tiles[0][0:64, r * 512:(r + 1) * 512], in_=PS[0:64, bank, :]),
                                    reads=[B(f"ps{bank}")], writes=[B(f"{nm}0_{p}")])
                                cx.op(dve, lambda: nc.vector.tensor_copy(
                                    out=tiles[1][64:128, r * 512:(r + 1) * 512], in_=PS[64:128, bank, :]),
                                    reads=[B(f"ps{bank}")], writes=[B(f"{nm}1_{p}")])
                            if SPLIT_K:
                                pcs.append(lambda mm=mm: mm(0))
                                pcs.append(lambda mm=mm, ev=ev: (mm(4), ev()))
                            else:
                                pcs.append(lambda mm=mm, ev=ev: (mm(0, 8), ev()))
                    for g in range(4):
                        bank = 6 + (cnt[0] % 2)
                        cnt[0] += 1
                        pv = psb(bank).rearrange("p (a c) -> p a c", a=4)
                        for a in range(4):
                            t = 4 * g + a

                            def mmv(a=a, t=t, pv=pv, bank=bank):
                                cx.group(pe, [lambda kc=kc: nc.tensor.matmul(
                                    pv[:, a, :], HT[:, kc, t * 128:(t + 1) * 128], WS[p][2][:, kc, :],
                                    start=(kc == 0), stop=(kc == KC - 1), skip_group_check=True)
                                    for kc in range(KC)],
                                    reads=[B(f"HT{t}"), B(f"ws{p}_2")], writes=[B(f"ps{bank}")])

                            def evv(g=g, pv=pv, bank=bank):
                                cx.op(dve, lambda: nc.vector.tensor_copy(
                                    out=VA[p][:, 4 * g:4 * g + 4, 0:128], in_=pv),
                                    reads=[B(f"ps{bank}")], writes=[B(f"VA_{p}")])
                            if a < 3:
                                pcs.append(mmv)
                            else:
                                pcs.append(lambda mmv=mmv, evv=evv: (mmv(), evv()))
                    return pcs

                def normalise(h, Q):
                    cx.op(dve, lambda: nc.vector.reciprocal(out=RZ[:], in_=OS[:, :, 128]),
                          reads=[B("OS")], writes=[B("RZ")])
                    cx.op(dve, lambda: nc.vector.tensor_scalar(
                        out=RZ[:, 2:4], in0=RZ[:, 2:4], scalar1=NEGLAM[:, j:j + 1], scalar2=None, op0=ALU.mult),
                        reads=[B("RZ")], writes=[B("RZ")])
                    for qs in range(2):
                        cx.op(dve, lambda qs=qs: nc.vector.tensor_scalar(
                            out=DD[:, qs, :], in0=OS[:, qs, 0:128], scalar1=RZ[:, qs:qs + 1], scalar2=None,
                            op0=ALU.mult), reads=[B("OS"), B("RZ")], writes=[B("DD")])
                        cx.op(dve, lambda qs=qs: nc.vector.scalar_tensor_tensor(
                            out=DD[:, qs, :], in0=OS[:, 2 + qs, 0:128], scalar=RZ[:, 2 + qs:3 + qs],
                            in1=DD[:, qs, :], op0=ALU.mult, op1=ALU.add),
                            reads=[B("OS"), B("RZ"), B("DD")], writes=[B("DD")])
                    cx.op(dve, lambda: nc.vector.tensor_tensor(out=DJ[:], in0=DD[:], in1=DD[:], op=ALU.mult),
                          reads=[B("DD")], writes=[B("DJ")])
                    cx.op(dve, lambda: nc.vector.tensor_reduce(
                        out=SSO[:], in_=DJ[:], axis=mybir.AxisListType.X, op=ALU.add),
                        reads=[B("DJ")], writes=[B("SSO")])
                    cx.op(act, lambda: nc.scalar.activation(
                        out=SSO[:], in_=SSO[:], func=AF.Ln, bias=EPSC[:, 0:1], scale=1.0 / 128),
                        reads=[B("SSO")], writes=[B("SSO")])
                    cx.op(act, lambda: nc.scalar.activation(out=SSO[:], in_=SSO[:], func=AF.Exp, scale=-0.5),
                          reads=[B("SSO")], writes=[B("SSO")])
                    for qs in range(2):
                        t = 2 * Q + qs
                        cx.op(dve, lambda qs=qs, t=t: nc.vector.scalar_tensor_tensor(
                            out=SG[:, t, h * 128:(h + 1) * 128], in0=DD[:, qs, :], scalar=SSO[:, qs:qs + 1],
                            in1=SG[:, t, h * 128:(h + 1) * 128], op0=ALU.mult, op1=ALU.mult),
                            reads=[B("DD"), B("SSO"), B(f"SG{t}")], writes=[B(f"SG{t}")])

                pending = []
                load_head_w(0)
                load_head_w(1)
                for pc in proj_pieces(0):
                    pc()
                load_wout(WBIG, awout_d[j], SCW, None)
                stream = []
                for h in range(H):
                    for Q in range(8):
                        nkt = 2 * Q + 2
                        for kt in range(nkt):
                            jd = kt - 2 * Q
                            c0 = 128 * jd if jd >= 0 else 0
                            stream.append((h, Q, kt, jd, c0, kt, nkt))
                NS = len(stream)

                def emit_S(n):
                    h, Q, kt, jd, c0, i, nkt = stream[n]
                    p = h % 2
                    QAs, KAs = QA[p], KA[p]
                    hb = [B(f"QA0_{p}"), B(f"QA1_{p}"), B(f"KA0_{p}"), B(f"KA1_{p}")]
                    sbank = SBK[n % 4]
                    ps_s = psb(sbank).rearrange("p (m c) -> p m c", m=2)
                    fns = []
                    for m in range(2):
                        kk = 68 if m == 0 else 128
                        fns.append(lambda m=m, kk=kk: nc.tensor.matmul(
                            ps_s[:, m, c0:256], KAs[m][0:kk, kt * 128:(kt + 1) * 128],
                            QAs[m][0:kk, Q * 256 + c0:(Q + 1) * 256],
                            start=True, stop=(jd < 0), skip_group_check=True))
                        if jd >= 0:
                            fns.append(lambda m=m: nc.tensor.matmul(
                                ps_s[:, m, c0:c0 + 128], MASKNEG[:], IDENT[:],
                                start=False, stop=True, skip_group_check=True))
                    cx.group(pe, fns, reads=hb + [B("consts")], writes=[B(f"ps{sbank}")])
                    et = ET[n % NET]
                    cx.op(act, lambda: nc.scalar.activation(
                        out=et[:, :, c0:256], in_=ps_s[:, :, c0:256], func=AF.Exp, scale=0.125),
                        reads=[B(f"ps{sbank}")], writes=[B(f"et{n % NET}")])

                def emit_PV(n):
                    h, Q, kt, jd, c0, i, nkt = stream[n]
                    p = h % 2
                    et = ET[n % NET]
                    fns = []
                    for m in range(2):
                        for qs in range(2):
                            if jd >= 0 and qs < jd:
                                continue
                            last = 2 * Q + qs
                            fns.append(lambda m=m, qs=qs, last=last: nc.tensor.matmul(
                                PS[:, 2 + m, qs * 129:(qs + 1) * 129], et[:, m, qs * 128:(qs + 1) * 128],
                                VA[p][:, kt, :], start=(kt == 0 and qs == 0), stop=(kt == last),
                                skip_group_check=True))
                    cx.group(pe, fns, reads=[B(f"et{n % NET}"), B(f"VA_{p}")], writes=[B("psO")])

                nxt = []
                nxt_head = [0]

                def ensure_proj(h):
                    while nxt:
                        nxt.pop(0)()
                    assert nxt_head[0] >= h

                for n in range(min(DEPTH, NS)):
                    emit_S(n)
                tile_no = 0
                for n in range(NS):
                    h, Q, kt, jd, c0, i, nkt = stream[n]
                    if i == 0 and Q == 0:
                        tile_no = 0
                        if h + 1 < H:
                            nxt.extend(proj_pieces(h + 1))
                            nxt_head[0] = h + 1
                    if n + DEPTH < NS:
                        h2 = stream[n + DEPTH][0]
                        if h2 != h:
                            ensure_proj(h2)
                        emit_S(n + DEPTH)
                    emit_PV(n)
                    if i == min(3, nkt - 1) and pending:
                        pending.pop()()
                    tile_no += 1
                    if INTERLEAVE and nxt and tile_no % 2 == 0:
                        nxt.pop(0)()
                    if i == nkt - 1:
                        cx.op(act, lambda: nc.scalar.copy(
                            out=OS[:].rearrange("p a c -> p (a c)").rearrange("p (m x) -> p m x", m=2),
                            in_=PS[:, 2:4, 0:258]), reads=[B("psO")], writes=[B("OS")])
                        pending.append(lambda h=h, Q=Q: normalise(h, Q))
                        if Q == 7:
                            while nxt:
                                nxt.pop(0)()
                            if h + 2 < H:
                                load_head_w(h + 2)
                pending.pop()()
                cx.barrier()

        def lru_layer(layer, prenormed=False, next_layer=None, store=None):
            j = layer // 2
            win = lwin_d[j].rearrange("(kc p) c -> p kc c", p=128)
            if not prenormed:
                prenorm(layer)
            with ExitStack() as es2:
                def sb2(name, shape, dtype):
                    return es2.enter_context(sbt(name, shape, dtype))
                WXs = [sb2(f"WX{i}", [128, KC, 256], BF16) for i in range(2)]
                WGT = sb2("WGT", [128, KC, 256], BF16)
                GA = sb2("GA", [128, 2, 256], BF16)
                GX = sb2("GX", [128, 2, 256], BF16)
                XR = sb2("XR", [128, 3 + S], F32)
                XC = sb2("XC", [128, 2, S], F32)
                XCB = sb2("XCB", [128, 2, S], BF16)
                TR = [sb2(f"TR{i}", [128, 512], F32) for i in range(2)]
                HS = [sb2(f"HS{i}", [128, 512], F32) for i in range(2)]
                TI = [sb2(f"TI{i}", [128, 512], F32) for i in range(4)]
                TG = [sb2(f"TG{i}", [128, 512], BF16) for i in range(4)]
                AA = [sb2(f"AA{i}", [128, 512], F32) for i in range(4)]
                A2 = [sb2(f"A2{i}", [128, 512], F32) for i in range(4)]
                cx.op(dve, lambda: nc.vector.memset(XR[:, 0:3], 0.0), writes=[B("XRpad")])
                it = 0
                def load_wx(g):
                    for k0 in range(0, KC, 4):
                        load_w(WXs[g % 2][:, k0:k0 + 4, :], win[:, k0:k0 + 4, g * 256:(g + 1) * 256], 4, 256,
                               eng=act, name=f"wx{g % 2}_{k0}")
                load_wx(0)
                for g in range(4):
                    WX = WXs[g % 2]
                    for k0 in range(0, KC, 4):
                        load_w(WGT[:, k0:k0 + 4, :], win[:, k0:k0 + 4, D + g * 256:D + (g + 1) * 256], 4, 256,
                               eng=act, name=f"wgt{k0}")
                    load_w(GA[:], gaw_d[j, g].rearrange("(ic p) o -> p ic o", p=128), 2, 256, eng=act, name="ga")
                    load_w(GX[:], gxw_d[j, g].rearrange("(ic p) o -> p ic o", p=128), 2, 256, eng=act, name="gx")
                    for cc in range(2):
                        c = 2 * g + cc
                        wcol = lambda tap: COLS[:, C_CONVW + (j * 4 + tap) * KC + c:C_CONVW + (j * 4 + tap) * KC + c + 1]
                        bcol = COLS[:, C_CONVB + j * KC + c:C_CONVB + j * KC + c + 1]
                        cast_pending = []
                        for r in range(4):
                            bank = 6 + (r % 2)
                            cx.group(pe, [lambda kc=kc, r=r, bank=bank, cc=cc: nc.tensor.matmul(
                                psb(bank), WX[:, kc, cc * 128:(cc + 1) * 128], HT[:, kc, r * 512:(r + 1) * 512],
                                start=(kc == 0), stop=(kc == KC - 1)) for kc in range(KC)],
                                reads=[B(f"HT{t}") for t in range(4 * r, 4 * r + 4)] + [B(f"wx{g % 2}_0"), B(f"wx{g % 2}_4")],
                                writes=[B(f"ps{bank}")])
                            xo = XC[:, cc, r * 512:(r + 1) * 512]
                            cx.op(act, lambda r=r, bank=bank: nc.scalar.copy(
                                out=XR[:, 3 + r * 512:3 + (r + 1) * 512], in_=psb(bank)),
                                reads=[B(f"ps{bank}")], writes=[B(f"XR{r}")])
                            cx.op(act, lambda bank=bank, xo=xo: nc.scalar.activation(
                                out=xo, in_=psb(bank), func=AF.Identity, bias=bcol, scale=wcol(3)),
                                reads=[B(f"ps{bank}")], writes=[B(f"XC{cc}_{r}")])
                            while cast_pending:
                                cast_pending.pop()()
                            rd = [B(f"XR{r}"), B("XRpad")] + ([B(f"XR{r - 1}")] if r > 0 else [])
                            for tap in range(3):
                                cx.op(dve, lambda r=r, xo=xo, tap=tap: nc.vector.scalar_tensor_tensor(
                                    out=xo, in0=XR[:, r * 512 + tap:(r + 1) * 512 + tap], scalar=wcol(tap), in1=xo,
                                    op0=ALU.mult, op1=ALU.add), reads=rd + [B(f"XC{cc}_{r}")],
                                    writes=[B(f"XC{cc}_{r}")])
                            cast_pending.append(lambda r=r, xo=xo, cc=cc: cx.op(act, lambda: nc.scalar.copy(
                                out=XCB[:, cc, r * 512:(r + 1) * 512], in_=xo),
                                reads=[B(f"XC{cc}_{r}")], writes=[B(f"XCB{cc}_{r}")]))
                        while cast_pending:
                            cast_pending.pop()()
                    if g + 1 < 4:
                        load_wx(g + 1)
                    for cc in range(2):
                        c = 2 * g + cc
                        cs = slice(cc * 128, (cc + 1) * 128)
                        for r in range(4):
                            q = it % 2
                            it += 1
                            rs = slice(r * 512, (r + 1) * 512)
                            ba, bi, bg = 3 * q, 3 * q + 1, 3 * q + 2
                            xcb_r = [B(f"XCB0_{r}"), B(f"XCB1_{r}")]
                            cx.group(pe, [lambda ic=ic: nc.tensor.matmul(
                                psb(ba), GA[:, ic, cs], XCB[:, ic, rs], start=(ic == 0), stop=(ic == 1))
                                for ic in range(2)], reads=xcb_r + [B("ga")], writes=[B(f"ps{ba}")])
                            cx.group(pe, [lambda ic=ic: nc.tensor.matmul(
                                psb(bi), GX[:, ic, cs], XCB[:, ic, rs], start=(ic == 0), stop=(ic == 1))
                                for ic in range(2)], reads=xcb_r + [B("gx")], writes=[B(f"ps{bi}")])
                            cx.group(pe, [lambda kc=kc: nc.tensor.matmul(
                                psb(bg), WGT[:, kc, cs], HT[:, kc, rs], start=(kc == 0), stop=(kc == KC - 1))
                                for kc in range(KC)],
                                reads=[B(f"HT{t}") for t in range(4 * r, 4 * r + 4)] + [B("wgt0"), B("wgt4")],
                                writes=[B(f"ps{bg}")])
                            cx.op(act, lambda: nc.scalar.activation(
                                out=TR[q][:], in_=psb(ba), func=AF.Tanh, bias=HBS[:, j, c, 0:1], scale=0.5),
                                reads=[B(f"ps{ba}"), B("hbs")], writes=[B(f"TR{q}")])
                            cx.op(act, lambda: nc.scalar.activation(
                                out=TI[r][:], in_=psb(bi), func=AF.Tanh, bias=HBS[:, j, c, 1:2], scale=0.5),
                                reads=[B(f"ps{bi}"), B("hbs")], writes=[B(f"TI{r}")])
                            cx.op(act, lambda: nc.scalar.activation(
                                out=TG[r][:], in_=psb(bg), func=AF.Tanh, scale=0.5),
                                reads=[B(f"ps{bg}")], writes=[B(f"TG{r}")])
                            cx.op(act, lambda: nc.scalar.activation(
                                out=AA[r][:], in_=TR[q][:], func=AF.Exp, bias=CCS[:, j, c, 0:1], scale=CCS[:, j, c, 0:1]),
                                reads=[B(f"TR{q}"), B("ccs")], writes=[B(f"AA{r}")])
                            cx.op(act, lambda: nc.scalar.activation(
                                out=A2[r][:], in_=TR[q][:], func=AF.Exp, bias=CCS[:, j, c, 1:2], scale=CCS[:, j, c, 1:2]),
                                reads=[B(f"TR{q}"), B("ccs")], writes=[B(f"A2{r}")])
                            cx.op(dve, lambda: nc.vector.scalar_tensor_tensor(
                                out=TG[r][:], in0=TG[r][:], scalar=1.0, in1=psb(bg), op0=ALU.add, op1=ALU.mult),
                                reads=[B(f"TG{r}"), B(f"ps{bg}")], writes=[B(f"TG{r}")])
                        for r in range(4):
                            cx.op(act, lambda: nc.scalar.activation(
                                out=A2[r][:], in_=A2[r][:], func=AF.Sqrt, bias=ONEC[:, 0:1], scale=-1.0),
                                reads=[B(f"A2{r}")], writes=[B(f"A2{r}")])
                        for r in range(4):
                            q = r % 2
                            rs = slice(r * 512, (r + 1) * 512)
                            cx.op(dve, lambda: nc.vector.scalar_tensor_tensor(
                                out=TI[r][:], in0=TI[r][:], scalar=1.0, in1=A2[r][:], op0=ALU.add, op1=ALU.mult),
                                reads=[B(f"TI{r}"), B(f"A2{r}")], writes=[B(f"TI{r}")])
                            cx.op(dve, lambda: nc.vector.tensor_tensor(
                                out=TI[r][:], in0=TI[r][:], in1=XC[:, cc, rs], op=ALU.mult),
                                reads=[B(f"TI{r}"), B(f"XC{cc}_{r}")], writes=[B(f"TI{r}")])
                            init = 0.0 if r == 0 else HS[1 - q][:, 511:512]
                            cx.op(dve, lambda init=init: nc.vector.tensor_tensor_scan(
                                out=HS[q][:], data0=AA[r][:], data1=TI[r][:], initial=init,
                                op0=ALU.mult, op1=ALU.add),
                                reads=[B(f"AA{r}"), B(f"TI{r}")] + ([B(f"HS{1 - q}")] if r > 0 else []),
                                writes=[B(f"HS{q}")])
                            cx.op(dve, lambda: nc.vector.scalar_tensor_tensor(
                                out=YGT[:, c, rs], in0=HS[q][:], scalar=0.25, in1=TG[r][:], op0=ALU.mult, op1=ALU.mult),
                                reads=[B(f"HS{q}"), B(f"TG{r}")], writes=[B(f"YG{c}")])
                cx.barrier()
            wb = None
            hook = None
            if next_layer is not None and next_layer % 2 == 0:
                wb = alloc_wbig()
                hook = lambda: load_gate_w(next_layer, wb[1])
            out_proj_post(layer, lwout_d[j], lambda t, kc: YGT[:, kc, t * 128:(t + 1) * 128],
                          lambda t: [B(f"YG{c}") for c in range(KC)], None, weng=act,
                          next_layer=next_layer, store=store, pre_hook=hook)
            return wb

        EPSC = sb("EPSC", [128, 1], F32)
        cx.op(dve, lambda: nc.vector.memset(EPSC[:], EPS), writes=[B("epsc")])
        cx.op(dve, lambda: nc.vector.memset(ONEC[:], 1.0), writes=[B("onec")])
        cx.barrier()
        for jj in range(2):
            la = COLS[:, C_LOGA + jj * KC:C_LOGA + (jj + 1) * KC]
            cx.op(act, lambda: nc.scalar.activation(out=CCS[:, jj, :, 0], in_=la, func=AF.Exp, scale=-1.0),
                  writes=[B("ccs")])
            cx.op(act, lambda: nc.scalar.activation(out=CCS[:, jj, :, 1], in_=CCS[:, jj, :, 0], func=AF.Ln,
                                                    bias=ONEC[:, 0:1], scale=1.0),
                  reads=[B("ccs")], writes=[B("ccs")])
            cx.op(dve, lambda: nc.vector.tensor_scalar(out=CCS[:, jj, :, 0], in0=CCS[:, jj, :, 1], scalar1=-4.0,
                                                       scalar2=None, op0=ALU.mult),
                  reads=[B("ccs")], writes=[B("ccs")])
            cx.op(dve, lambda: nc.vector.tensor_scalar(out=CCS[:, jj, :, 1], in0=CCS[:, jj, :, 1], scalar1=-8.0,
                                                       scalar2=None, op0=ALU.mult),
                  reads=[B("ccs")], writes=[B("ccs")])
            cx.op(dve, lambda: nc.vector.tensor_scalar(out=HBS[:, jj, :, 0], in0=COLS[:, C_GAB + jj * KC:C_GAB + (jj + 1) * KC],
                                                       scalar1=0.5, scalar2=None, op0=ALU.mult), writes=[B("hbs")])
            cx.op(dve, lambda: nc.vector.tensor_scalar(out=HBS[:, jj, :, 1], in0=COLS[:, C_GXB + jj * KC:C_GXB + (jj + 1) * KC],
                                                       scalar1=0.5, scalar2=None, op0=ALU.mult), writes=[B("hbs")])
        cx.barrier()
        def xin(s, g):
            xv = x_d[s].rearrange("(t p) d -> p t d", p=128)
            cx.dma(cx.dsem(f"xin{g}"), [(X[:, 4 * g:4 * g + 4, :], xv[:, 4 * g:4 * g + 4, :])],
                   writes=[B(f"X{t}") for t in range(4 * g, 4 * g + 4)])

        for g in range(4):
            xin(0, g)
        for s in range(nseq):
            yv = y_d[s].rearrange("(t p) d -> p t d", p=128)

            def store(t, s=s, yv=yv):
                g = t // 4
                cx.dma(cx.dsem(f"yout{t}"), [(yv[:, t, :], X[:, t, :])], reads=[B(f"X{t}")])
                if t % 4 == 3 and s + 1 < nseq:
                    xin(s + 1, g)

            prenormed = False
            wb = None
            for idx, layer in enumerate(layers):
                nxt = layers[idx + 1] if idx + 1 < len(layers) else None
                st = store if nxt is None else None
                if layer % 2 == 0:
                    attn_layer(layer, prenormed, wb, nxt, st)
                    wb = None
                else:
                    wb = lru_layer(layer, prenormed, nxt, st)
                prenormed = nxt is not None
            cx.barrier()
        for t in range(NT):
            d = cx.dsems[f"yout{t}"]
            nc.sync.wait_ge(d.sem, d.val)
    return nc


def _host_consts():
    ident = np.eye(128, dtype=np.float32).astype(ml_dtypes.bfloat16)
    pos = np.arange(S)
    a128 = (pos // 128 * 128).astype(np.float32)
    b = (pos % 128).astype(np.float32)
    one = np.ones(S, np.float32)
    augq = np.zeros((H, 4, S), np.float32)
    augk = np.zeros((H, 4, S), np.float32)
    for h in range(H):
        s8 = 8.0 * 2.0 ** (-(h + 1))
        augk[h, 0], augq[h, 0] = s8 * one, -a128
        augk[h, 1], augq[h, 1] = s8 * one, -b
        augk[h, 2], augq[h, 2] = a128, s8 * one
        augk[h, 3], augq[h, 3] = b, s8 * one
    return ident, augq.astype(ml_dtypes.bfloat16), augk.astype(ml_dtypes.bfloat16)


def _pack_cols(inp):
    cols = np.zeros((128, C_TOTAL), np.float32)

    def colmajor(v):
        v = np.asarray(v, np.float32)
        lead = v.shape[:-1]
        return np.moveaxis(v.reshape(*lead, KC, 128), -1, 0).reshape(128, -1)

    cols[:, C_PREW:C_PREW + 32] = colmajor(inp["pre_norm_w"])
    cols[:, C_SUBLN:C_SUBLN + 2] = np.asarray(inp["attn_subln_w"], np.float32).T
    cols[:, C_CONVW:C_CONVW + 64] = colmajor(inp["lru_conv_w"])
    cols[:, C_CONVB:C_CONVB + 16] = colmajor(inp["lru_conv_b"])
    cols[:, C_GAB:C_GAB + 16] = colmajor(inp["lru_gate_a_b"])
    cols[:, C_GXB:C_GXB + 16] = colmajor(inp["lru_gate_x_b"])
    cols[:, C_LOGA:C_LOGA + 16] = colmajor(inp["lru_log_a_param"])
    return cols


def make_in_maps(inp, n_cores=N_CORES):
    f = lambda k: np.ascontiguousarray(np.asarray(inp[k], np.float32))
    ident, augq, augk = _host_consts()
    cols = _pack_cols(inp)
    postw = np.ascontiguousarray(np.broadcast_to(f("post_norm_w")[:, None, :], (4, 128, D)))
    prewr = np.ascontiguousarray(np.broadcast_to(f("pre_norm_w")[:, None, :], (4, 128, D)))
    kk = np.arange(128)
    tri1 = (kk[None, :] >= kk[:, None]).astype(np.float32)
    tri = np.ascontiguousarray(np.broadcast_to(tri1[:, None, :], (128, 2, 128))).astype(ml_dtypes.bfloat16)
    maskneg = np.where(kk[None, :] > kk[:, None], -30000.0, 0.0).astype(np.float32).astype(ml_dtypes.bfloat16)
    lam = np.stack([f("attn_lambda_q1"), f("attn_lambda_k1"), f("attn_lambda_q2"), f("attn_lambda_k2")], axis=1)
    lamrows = np.ascontiguousarray(np.broadcast_to(lam[None], (128, 2, 4, 64)))
    x = f("x")
    shared = {
        "attn_w_in": f("attn_w_in"), "attn_w_out": f("attn_w_out"),
        "lru_w_in": f("lru_w_in"), "lru_w_out": f("lru_w_out"),
        "lru_gate_a_w": f("lru_gate_a_w"), "lru_gate_x_w": f("lru_gate_x_w"),
        "cols": cols, "postw": postw, "prewr": prewr, "tri": tri, "maskneg": maskneg, "lamrows": lamrows,
        "ident": ident, "augq": augq, "augk": augk,
    }
    maps = []
    for c in range(n_cores):
        m = dict(shared)
        m["x"] = np.ascontiguousarray(x[c * SEQ_PER_CORE:(c + 1) * SEQ_PER_CORE])
        maps.append(m)
    return maps


def kernel(**inputs):
    nc = build_nc()
    in_maps = make_in_maps(inputs)
    res = run_bass_kernel_spmd(nc, in_maps, core_ids=list(range(N_CORES)))
    return np.concatenate([np.asarray(r["y"], np.float32) for r in res.results], axis=0)
```
